# Optimizing a Trainium2 kernel written in Bass

```python
import jax, jax.numpy as jnp
from jax import lax
import numpy as np

D_MODEL = 1024
BATCH = 4
SEQ = 4096
DEPTH = 2

CHUNK = 64
D_RNN = 1280
RNN_HEADS = 20
RNN_HEAD_DIM = D_RNN // RNN_HEADS
CONV_WIDTH = 4
LRU_C = 8.0
D_SGU = 1024
SGU_GROUPS = 8
SGU_GROUP_DIM = D_SGU // SGU_GROUPS
SGU_BLOCK = 128
N_BRANCH = 2
D_FF = 4 * D_MODEL
D_IN = 2 * D_RNN + 2 * D_SGU + N_BRANCH * D_MODEL
EPS = 1e-6

kernel_name = "hybrid_rglru_sgu_gated_encoder"


def rmsnorm(x, g):
    xf = x.astype(jnp.float32)
    y = xf * lax.rsqrt(jnp.mean(xf * xf, axis=-1, keepdims=True) + EPS)
    return (y * g.astype(jnp.float32)).astype(x.dtype)


def layernorm(x, g, b):
    xf = x.astype(jnp.float32)
    mu = jnp.mean(xf, axis=-1, keepdims=True)
    xc = xf - mu
    y = xc * lax.rsqrt(jnp.mean(xc * xc, axis=-1, keepdims=True) + EPS)
    return (y * g.astype(jnp.float32) + b.astype(jnp.float32)).astype(x.dtype)


def causal_depthwise_conv(x, w, b):
    s = x.shape[1]
    xp = jnp.pad(x, ((0, 0), (CONV_WIDTH - 1, 0), (0, 0)))
    y = b
    for k in range(CONV_WIDTH):
        y = y + xp[:, k:k + s, :] * w[k]
    return y


def rg_lru(x, w_a, b_a, w_x, b_x, lam):
    bsz, s, _ = x.shape
    xh = x.reshape(bsz, s, RNN_HEADS, RNN_HEAD_DIM)
    r = jax.nn.sigmoid(jnp.einsum('bshi,hij->bshj', xh, w_a) + b_a).reshape(bsz, s, D_RNN)
    i = jax.nn.sigmoid(jnp.einsum('bshi,hij->bshj', xh, w_x) + b_x).reshape(bsz, s, D_RNN)
    log_a = (-LRU_C * r.astype(jnp.float32)) * jax.nn.softplus(-lam.astype(jnp.float32))
    a = jnp.exp(log_a)
    norm = jnp.sqrt(-jnp.expm1(2.0 * log_a))
    u = norm * (i * x).astype(jnp.float32)

    def combine(left, right):
        a_l, b_l = left
        a_r, b_r = right
        return a_l * a_r, a_r * b_l + b_r

    _, h = lax.associative_scan(combine, (a, u), axis=1)
    return h.astype(x.dtype)


def spatial_gating(u, v, w_s, b_s, ln_g, ln_b):
    bsz, s, _ = u.shape
    nblk = s // SGU_BLOCK
    v = layernorm(v, ln_g, ln_b)
    vb = v.reshape(bsz, nblk, SGU_BLOCK, SGU_GROUPS, SGU_GROUP_DIM)
    chunk_id = jnp.arange(SGU_BLOCK) // CHUNK
    mask = (chunk_id[:, None] >= chunk_id[None, :]).astype(w_s.dtype)
    mixed = jnp.einsum('gts,bnsgc->bntgc', w_s * mask, vb)
    mixed = mixed + jnp.transpose(b_s)[None, None, :, :, None]
    return u * mixed.reshape(bsz, s, D_SGU)


def hybrid_layer(x, norm_mix_g, w_in, conv_w, conv_b, lru_w_a, lru_b_a, lru_w_x, lru_b_x,
                 lru_lambda, sgu_ln_g, sgu_ln_b, sgu_w_s, sgu_b_s, w_branch_a, w_branch_b,
                 w_out, norm_ffn_g, w_up, w_down):
    h = rmsnorm(x, norm_mix_g)
    proj = jnp.einsum('bsd,de->bse', h, w_in)
    cuts = np.cumsum([D_RNN, D_RNN, D_SGU, D_SGU, D_MODEL])
    x_rnn, g_rnn, u, v, gate_a, gate_b = jnp.split(proj, cuts, axis=-1)

    xr = causal_depthwise_conv(x_rnn, conv_w, conv_b)
    ya = rg_lru(xr, lru_w_a, lru_b_a, lru_w_x, lru_b_x, lru_lambda) * jax.nn.gelu(g_rnn)
    ya = jnp.einsum('bsr,rd->bsd', ya, w_branch_a)

    yb = spatial_gating(jax.nn.gelu(u), jax.nn.gelu(v), sgu_w_s, sgu_b_s, sgu_ln_g, sgu_ln_b)
    yb = jnp.einsum('bsc,cd->bsd', yb, w_branch_b)

    merged = jax.nn.sigmoid(gate_a) * ya + jax.nn.sigmoid(gate_b) * yb
    x = x + jnp.einsum('bsd,de->bse', merged, w_out)

    h2 = rmsnorm(x, norm_ffn_g)
    f = jnp.square(jax.nn.relu(jnp.einsum('bsd,df->bsf', h2, w_up)))
    return x + jnp.einsum('bsf,fd->bsd', f, w_down)


def setup_inputs(seed: int = 0) -> dict:
    key = jax.random.key(seed)
    ks = jax.random.split(key, 24)
    f32 = jnp.float32

    def nrm(k, shape, scale):
        return jax.random.normal(k, shape, f32) * scale

    a_c = jax.random.uniform(ks[7], (DEPTH, D_RNN), f32, 0.9, 0.999)
    a0 = a_c ** (1.0 / LRU_C)
    lru_lambda = jnp.log(a0) - jnp.log1p(-a0)

    return {
        "x": nrm(ks[0], (BATCH, SEQ, D_MODEL), 1.0),
        "norm_mix_g": 1.0 + nrm(ks[1], (DEPTH, D_MODEL), 0.02),
        "w_in": nrm(ks[2], (DEPTH, D_MODEL, D_IN), D_MODEL ** -0.5),
        "conv_w": nrm(ks[3], (DEPTH, CONV_WIDTH, D_RNN), CONV_WIDTH ** -0.5),
        "conv_b": nrm(ks[4], (DEPTH, D_RNN), 0.02),
        "lru_w_a": nrm(ks[5], (DEPTH, RNN_HEADS, RNN_HEAD_DIM, RNN_HEAD_DIM), RNN_HEAD_DIM ** -0.5),
        "lru_b_a": nrm(ks[6], (DEPTH, RNN_HEADS, RNN_HEAD_DIM), 0.02),
        "lru_w_x": nrm(ks[8], (DEPTH, RNN_HEADS, RNN_HEAD_DIM, RNN_HEAD_DIM), RNN_HEAD_DIM ** -0.5),
        "lru_b_x": nrm(ks[9], (DEPTH, RNN_HEADS, RNN_HEAD_DIM), 0.02),
        "lru_lambda": lru_lambda,
        "sgu_ln_g": 1.0 + nrm(ks[10], (DEPTH, D_SGU), 0.02),
        "sgu_ln_b": nrm(ks[11], (DEPTH, D_SGU), 0.02),
        "sgu_w_s": nrm(ks[12], (DEPTH, SGU_GROUPS, SGU_BLOCK, SGU_BLOCK), SGU_BLOCK ** -0.5),
        "sgu_b_s": 1.0 + nrm(ks[13], (DEPTH, SGU_GROUPS, SGU_BLOCK), 0.02),
        "w_branch_a": nrm(ks[14], (DEPTH, D_RNN, D_MODEL), D_RNN ** -0.5),
        "w_branch_b": nrm(ks[15], (DEPTH, D_SGU, D_MODEL), D_SGU ** -0.5),
        "w_out": nrm(ks[16], (DEPTH, D_MODEL, D_MODEL), D_MODEL ** -0.5),
        "norm_ffn_g": 1.0 + nrm(ks[17], (DEPTH, D_MODEL), 0.02),
        "w_up": nrm(ks[18], (DEPTH, D_MODEL, D_FF), D_MODEL ** -0.5),
        "w_down": nrm(ks[19], (DEPTH, D_FF, D_MODEL), D_FF ** -0.5),
        "final_norm_g": 1.0 + nrm(ks[20], (D_MODEL,), 0.02),
    }


def reference(x, norm_mix_g, w_in, conv_w, conv_b, lru_w_a, lru_b_a, lru_w_x, lru_b_x,
              lru_lambda, sgu_ln_g, sgu_ln_b, sgu_w_s, sgu_b_s, w_branch_a, w_branch_b,
              w_out, norm_ffn_g, w_up, w_down, final_norm_g):
    for l in range(DEPTH):
        x = hybrid_layer(x, norm_mix_g[l], w_in[l], conv_w[l], conv_b[l], lru_w_a[l], lru_b_a[l],
                         lru_w_x[l], lru_b_x[l], lru_lambda[l], sgu_ln_g[l], sgu_ln_b[l],
                         sgu_w_s[l], sgu_b_s[l], w_branch_a[l], w_branch_b[l], w_out[l],
                         norm_ffn_g[l], w_up[l], w_down[l])
    return rmsnorm(x, final_norm_g)
```

```python
import numpy as np
import concourse.bass as bass
import concourse.mybir as mybir
from concourse.bass_utils import run_bass_kernel_spmd

F32 = mybir.dt.float32
BF16 = mybir.dt.bfloat16
AF = mybir.ActivationFunctionType
ALU = mybir.AluOpType

D = 1024
DR = 1280
DS = 1024
DFF = 4096
DIN = 6656
NL = 2
EPS = 1e-6
TG = 512
NPV = 120
ENGS = ("pe", "act", "dve", "pool", "sp")


class Sched:
    def __init__(self, nc):
        self.nc = nc
        self.q = {e: [] for e in ENGS}
        self.sems = {}
        self.cnt = {}
        self.seen = {e: {} for e in ENGS}
        self.last_w = {}
        self.readers = {}

    def sem(self, key):
        if key not in self.sems:
            self.sems[key] = self.nc.alloc_semaphore(name="s_" + key)
            self.cnt[key] = 0
        return self.sems[key]

    def _deps(self, engine, reads, writes):
        best = {}
        def add(ev):
            sk, v = ev
            if engine == "pe" and sk == "pe":
                return
            if best.get(sk, 0) < v:
                best[sk] = v
        for r in reads:
            ev = self.last_w.get(r)
            if ev is not None:
                add(ev)
        for w in writes:
            ev = self.last_w.get(w)
            if ev is not None:
                add(ev)
            for ev in self.readers.get(w, ()):
                add(ev)
        waits = []
        for sk, v in best.items():
            if self.seen[engine].get(sk, 0) < v:
                self.seen[engine][sk] = v
                waits.append((sk, v))
        return waits

    def _record(self, ev, reads, writes):
        for r in reads:
            self.readers.setdefault(r, []).append(ev)
        for w in writes:
            self.last_w[w] = ev
            self.readers[w] = []

    def op(self, engine, fn, reads=(), writes=(), inc=True):
        self.sem(engine)
        waits = self._deps(engine, reads, writes)
        ev = (engine, self.cnt[engine] + 1)
        if inc:
            self.cnt[engine] += 1
        else:
            assert engine == "pe"
        self._record(ev, reads, writes)
        self.q[engine].append((waits, fn, (engine, 1) if inc else None))

    def dma(self, queue, semkey, fn, reads=(), writes=()):
        self.sem(semkey)
        waits = self._deps(queue, reads, writes)
        self.cnt[semkey] += 16
        ev = (semkey, self.cnt[semkey])
        self._record(ev, reads, writes)
        self.q[queue].append((waits, fn, (semkey, 16)))

    def finish(self, dma_semkeys):
        waits = [(sk, self.cnt[sk]) for sk in dma_semkeys if sk in self.cnt]
        self.q["sp"].append((waits, None, None))

    def emit(self):
        nc = self.nc
        sems = self.sems

        def run(eng, items):
            for waits, fn, inc in items:
                for sk, v in waits:
                    eng.wait_ge(sems[sk], v)
                if fn is None:
                    continue
                ins = fn(eng)
                if inc is not None:
                    ins.then_inc(sems[inc[0]], inc[1])

        with nc.Block() as block:
            @block.sync
            def _(e):
                run(e, self.q["sp"])

            @block.tensor
            def _(e):
                run(e, self.q["pe"])

            @block.scalar
            def _(e):
                run(e, self.q["act"])

            @block.vector
            def _(e):
                run(e, self.q["dve"])

            @block.gpsimd
            def _(e):
                run(e, self.q["pool"])


class Region:
    def __init__(self, nc, name, nbytes):
        self.name = name
        self.t32 = nc.alloc_sbuf_tensor("rg_" + name, [128, nbytes // 4], F32).ap()
        self.t16 = self.t32.bitcast(BF16)

    def _res(self, off, nb):
        return [f"{self.name}.{u}" for u in range(off // 1024, (off + nb + 1023) // 1024)]

    def f32(self, off, n):
        return self.t32[:, off // 4: off // 4 + n], self._res(off, n * 4)

    def bf16(self, off, n):
        return self.t16[:, off // 2: off // 2 + n], self._res(off, n * 2)


def build(T, plan=None):
    dry = plan is None
    NG = T // TG
    nc = bass.Bass("TRN2", target_bir_lowering=False)
    S = Sched(nc)

    def din(name, shape):
        return nc.dram_tensor(name, shape, F32, kind="ExternalInput").ap()

    xT_d = din("xT", [D, T])
    win_d = din("w_in", [NL, D, DIN])
    wa_d = din("w_branch_a", [NL, DR, D])
    wb_d = din("w_branch_b", [NL, DS, D])
    wo_d = din("w_out", [NL, D, D])
    wu_d = din("w_up", [NL, D, DFF])
    wd_d = din("w_down", [NL, DFF, D])
    pv_d = din("pv", [NL, 128, NPV])
    gw_d = din("gw", [NL, 128, 2 * 10 * 128])
    ws_d = din("wsT", [NL, 128, 8 * 128])
    bsb_d = din("bsb", [NL, 128, 8 * 128])
    yT_d = nc.dram_tensor("yT", [D, T], F32, kind="ExternalOutput").ap()
    xs_d = nc.dram_tensor("xs", [D, T], F32, kind="Internal").ap()

    def sb(name, shape, dt=F32):
        return nc.alloc_sbuf_tensor("sb_" + name, shape, dt).ap()

    ones32 = sb("ones32", [128, 128])
    onesbf = sb("onesbf", [128, 128], BF16)
    epsb = sb("epsb", [128, 1])
    pv = sb("pv", [128, NL, NPV])
    sct = sb("sct", [128, NL, 10])
    sc = sb("sc", [128, NL, 10])
    sc2 = sb("sc2", [128, NL, 10])
    gwbf = sb("gwbf", [128, 2, 10, 128], BF16)
    wsbf = sb("wsbf", [128, 8, 128], BF16)
    bsb = sb("bsb", [128, 8, 128])
    Cg = sb("Cg", [128, 8, 128])
    halo = sb("halo", [128, 10, 3])
    hstate = sb("hstate", [128, 10])
    xg = sb("xg", [128, 8, TG])
    h = sb("h", [128, 8, TG], BF16)
    ya_pre = sb("ya_pre", [128, 10, TG], BF16)
    yb_pre = sb("yb_pre", [128, 8, TG], BF16)
    rstd = sb("rstd", [128, TG])
    t1 = sb("t1", [128, TG])
    acc = [sb(f"acc{i}", [128, TG]) for i in range(2)]
    xr_pad = [sb(f"xrp{i}", [128, TG + 3]) for i in range(2)]
    xr_bf = [sb(f"xrb{i}", [128, TG], BF16) for i in range(2)]
    rl = [sb(f"rl{i}", [128, TG]) for i in range(2)]
    tmpm = [sb(f"tmpm{i}", [128, TG]) for i in range(2)]
    bnst = sb("bnst", [128, 4, 12])
    mv = sb("mv", [128, 4, 2])
    lsd = sb("lsd", [128, 4])
    lrs = sb("lrs", [128, 4])
    A1 = Region(nc, "A1", 16384)
    A2 = Region(nc, "A2", 16384)
    A3 = Region(nc, "A3", 8192)
    FR = Region(nc, "FR", 32768)
    WCAP = 5120
    wst = [sb(f"wst{i}", [128, WCAP], BF16) for i in range(3)]
    pss = [nc.alloc_psum_tensor(f"ps{i}", [128, TG], F32).ap() for i in range(8)]
    psn = [0]

    def nextps():
        i = psn[0] % 8
        psn[0] += 1
        return pss[i], f"ps{i}"

    def MM(out, lhsT, rhs, start, stop, reads, writes, inc=None):
        if inc is None:
            inc = stop
        S.op("pe", lambda e: e.matmul(out, lhsT=lhsT, rhs=rhs, start=start, stop=stop),
             reads, writes, inc=inc)

    def ACT(out, in_, func, reads, writes, **kw):
        S.op("act", lambda e: e.activation(out=out, in_=in_, func=func, **kw), reads, writes)

    def TT(out, in0, in1, op, reads, writes):
        S.op("dve", lambda e: e.tensor_tensor(out=out, in0=in0, in1=in1, op=op), reads, writes)

    def TS(out, in0, s1, s2, op0, op1, reads, writes):
        S.op("dve", lambda e: e.tensor_scalar(out=out, in0=in0, scalar1=s1, scalar2=s2, op0=op0, op1=op1),
             reads, writes)

    def STT(out, in0, scalar, in1, op0, op1, reads, writes):
        S.op("dve", lambda e: e.scalar_tensor_tensor(out=out, in0=in0, scalar=scalar, in1=in1, op0=op0, op1=op1),
             reads, writes)

    def CP(out, in_, reads, writes):
        S.op("dve", lambda e: e.tensor_copy(out=out, in_=in_), reads, writes)

    def MSET(ap, val, writes):
        S.op("dve", lambda e: e.memset(ap, val), (), writes)

    recorded = []
    wstate = {"n": 0, "issued": 0}

    def issue(n):
        pieces, kt, ncols = plan[n]
        bi = n % 3
        view = wst[bi][:, 0:kt * ncols].rearrange("p (k c) -> p k c", k=kt)
        c0 = 0
        for (src2d, nc_) in pieces:
            dst = view[:, :, c0:c0 + nc_]
            src = src2d.rearrange("(k p) c -> p k c", p=128)
            S.dma("pool", f"w{bi}", lambda e, d=dst, s=src: e.dma_start(out=d, in_=s), (), [f"wst{bi}"])
            c0 += nc_

    def wget(pieces, kt):
        ncols = sum(p[1] for p in pieces)
        assert kt * ncols <= WCAP
        n = wstate["n"]
        wstate["n"] += 1
        if dry:
            recorded.append((pieces, kt, ncols))
            bi = n % 3
        else:
            while wstate["issued"] < min(n + 3, len(plan)):
                issue(wstate["issued"])
                wstate["issued"] += 1
            bi = n % 3
        view = wst[bi][:, 0:kt * ncols].rearrange("p (k c) -> p k c", k=kt)

        def W(k, c0, n_):
            return view[:, k, c0:c0 + n_]
        return W, f"wst{bi}"

    def pvc(l, col):
        return pv[:, l, col:col + 1]

    MSET(ones32, 1.0, ["ones32"])
    MSET(onesbf, 1.0, ["onesbf"])
    MSET(epsb, EPS, ["epsb"])
    S.dma("sp", "ld_pv", lambda e: e.dma_start(out=pv, in_=pv_d.rearrange("l p n -> p l n")), (), ["pv"])
    ACT(sct, pv[:, :, 78:88], AF.Exp, ["pv"], ["sct"], scale=-1.0)
    ACT(sct, sct, AF.Ln, ["sct"], ["sct"], scale=1.0, bias=1.0)
    S.op("dve", lambda e: e.tensor_scalar_mul(out=sc, in0=sct, scalar1=-8.0), ["sct"], ["sc"])
    S.op("dve", lambda e: e.tensor_scalar_mul(out=sc2, in0=sct, scalar1=-16.0), ["sct"], ["sc2"])

    def rmsnorm(l, gbase):
        for k in range(8):
            sqk, r = A1.f32(k * 2048, TG)
            ACT(sqk, xg[:, k, :], AF.Square, [f"xg{k}"], r)
        ps, pr = nextps()
        for k in range(8):
            sqk, r = A1.f32(k * 2048, TG)
            MM(ps, ones32, sqk, k == 0, k == 7, ["ones32"] + r, [pr])
        ACT(t1, ps, AF.Sqrt, [pr, "epsb"], ["t1"], scale=1.0 / D, bias=epsb)
        S.op("dve", lambda e: e.reciprocal(out=rstd, in_=t1), ["t1"], ["rstd"])
        if gbase is not None:
            for k in range(8):
                STT(h[:, k, :], xg[:, k, :], pvc(l, gbase + k), rstd, ALU.mult, ALU.mult,
                    [f"xg{k}", "pv", "rstd"], [f"h{k}"])

    hres = [f"h{k}" for k in range(8)]

    for l in range(NL):
        S.dma("pool", "ld_gw", lambda e, l=l: e.dma_start(
            out=gwbf.rearrange("p a c m -> p (a c m)"), in_=gw_d[l]), (), ["gwbf"])
        S.dma("pool", "ld_ws", lambda e, l=l: e.dma_start(
            out=wsbf.rearrange("p g t -> p (g t)"), in_=ws_d[l]), (), ["wsbf"])
        S.dma("sp", "ld_bsb", lambda e, l=l: e.dma_start(
            out=bsb.rearrange("p g t -> p (g t)"), in_=bsb_d[l]), (), ["bsb"])
        MSET(wsbf[64:128, :, 0:64], 0.0, ["wsbf"])
        for g in range(8):
            ps, pr = nextps()
            MM(ps[:, 0:128], onesbf, wsbf[:, g, :], True, True, ["onesbf", "wsbf"], [pr])
            STT(Cg[:, g, :], ps[:, 0:128], pvc(l, 96 + g), bsb[:, g, :], ALU.mult, ALU.add,
                [pr, "pv", "bsb"], ["Cg"])
        MSET(halo, 0.0, ["halo"])
        MSET(hstate, 0.0, ["hstate"])
        win = win_d[l]

        for g in range(NG):
            cols = slice(g * TG, (g + 1) * TG)
            xres = [f"xg{k}" for k in range(8)]
            src = (xT_d if l == 0 else xs_d).rearrange("(k p) t -> p k t", p=128)[:, :, cols]
            S.dma("sp", "ld_x", lambda e, s=src: e.dma_start(out=xg, in_=s),
                  ([] if l == 0 else [f"xs{g}"]), xres)
            rmsnorm(l, 0)

            for cp_ in range(5):
                W, wr = wget([(win[:, cp_ * 256: cp_ * 256 + 256], 256),
                              (win[:, DR + cp_ * 256: DR + cp_ * 256 + 256], 256)], 8)
                for ci in range(2):
                    c = 2 * cp_ + ci
                    par = c % 2
                    def tmp(idx, par=par):
                        return FR.f32((par * 8 + idx) * 2048, TG)
                    r_, r_r = tmp(0); i_, i_r = tmp(1); a_, a_r = tmp(2); a2_, a2_r = tmp(3)
                    nrm, nrm_r = tmp(4); u_, u_r = tmp(5); hs, hs_r = tmp(6); gg, gg_r = tmp(7)
                    xp = xr_pad[par]; xpr = f"xrp{par}"
                    ac = acc[par]; acr = f"acc{par}"
                    xb = xr_bf[par]; xbr = f"xrb{par}"
                    ps1, p1r = nextps()
                    for k in range(8):
                        MM(ps1, W(k, ci * 128, 128), h[:, k, :], k == 0, k == 7, [wr, f"h{k}"], [p1r])
                    ACT(xp[:, 3:TG + 3], ps1, AF.Copy, [p1r], [xpr])
                    CP(xp[:, 0:3], halo[:, c, :], ["halo"], [xpr])
                    TS(ac, xp[:, 3:TG + 3], pvc(l, 8 + 30 + c), pvc(l, 48 + c), ALU.mult, ALU.add,
                       [xpr, "pv"], [acr])
                    for tap in (2, 1, 0):
                        STT(ac, xp[:, tap:tap + TG], pvc(l, 8 + tap * 10 + c), ac, ALU.mult, ALU.add,
                            [xpr, "pv", acr], [acr])
                    CP(halo[:, c, :], xp[:, TG:TG + 3], [xpr], ["halo"])
                    ACT(xb, ac, AF.Copy, [acr], [xbr])
                    psa, par_ = nextps()
                    MM(psa, gwbf[:, 0, c, :], xb, True, True, ["gwbf", xbr], [par_])
                    psx, pxr = nextps()
                    MM(psx, gwbf[:, 1, c, :], xb, True, True, ["gwbf", xbr], [pxr])
                    ACT(r_, psa, AF.Sigmoid, [par_, "pv"], r_r, bias=pvc(l, 58 + c))
                    ACT(i_, psx, AF.Sigmoid, [pxr, "pv"], i_r, bias=pvc(l, 68 + c))
                    ACT(a_, r_, AF.Exp, r_r + ["sc"], a_r, scale=sc[:, l, c:c + 1])
                    ACT(a2_, r_, AF.Exp, r_r + ["sc2"], a2_r, scale=sc2[:, l, c:c + 1])
                    ACT(nrm, a2_, AF.Sqrt, a2_r, nrm_r, scale=-1.0, bias=1.0)
                    TT(u_, nrm, i_, ALU.mult, nrm_r + i_r, u_r)
                    TT(u_, u_, ac, ALU.mult, u_r + [acr], u_r)
                    S.op("dve", lambda e, hs=hs, a_=a_, u_=u_, c=c: e.tensor_tensor_scan(
                        out=hs, data0=a_, data1=u_, initial=hstate[:, c:c + 1], op0=ALU.mult, op1=ALU.add),
                        a_r + u_r + ["hstate"], hs_r)
                    CP(hstate[:, c:c + 1], hs[:, TG - 1:TG], hs_r, ["hstate"])
                    psg, pgr = nextps()
                    for k in range(8):
                        MM(psg, W(k, 256 + ci * 128, 128), h[:, k, :], k == 0, k == 7, [wr, f"h{k}"], [pgr])
                    ACT(gg, psg, AF.Gelu_apprx_tanh, [pgr], gg_r)
                    TT(ya_pre[:, c, :], hs, gg, ALU.mult, hs_r + gg_r, [f"ya{c}"])

            for ub in range(2):
                W, wr = wget([(win[:, 2560 + ub * 512: 2560 + (ub + 1) * 512], 512)], 8)
                for ji in range(4):
                    j = 4 * ub + ji
                    guj, gur = A1.f32(j * 2048, TG)
                    ps, pr = nextps()
                    for k in range(8):
                        MM(ps, W(k, ji * 128, 128), h[:, k, :], k == 0, k == 7, [wr, f"h{k}"], [pr])
                    ACT(guj, ps, AF.Gelu_apprx_tanh, [pr], gur)
            for vb in range(2):
                W, wr = wget([(win[:, 3584 + vb * 512: 3584 + (vb + 1) * 512], 512)], 8)
                for tt in range(4):
                    gvt, gvr = A2.f32(tt * 4096 + vb * 2048, 512)
                    ps, pr = nextps()
                    for k in range(8):
                        MM(ps, h[:, k, tt * 128:(tt + 1) * 128], W(k, 0, 512), k == 0, k == 7,
                           [wr, f"h{k}"], [pr])
                    ACT(gvt, ps, AF.Gelu_apprx_tanh, [pr], gvr)
            for tt in range(4):
                gvt, gvr = A2.f32(tt * 4096, 1024)
                vnt, vnr = A3.bf16(tt * 2048, 1024)
                S.op("dve", lambda e, tt=tt, gvt=gvt: e.bn_stats(out=bnst[:, tt, 0:6], in_=gvt[:, 0:512]),
                     gvr, [f"bnst{tt}"])
                S.op("dve", lambda e, tt=tt, gvt=gvt: e.bn_stats(out=bnst[:, tt, 6:12], in_=gvt[:, 512:1024]),
                     gvr, [f"bnst{tt}"])
                S.op("dve", lambda e, tt=tt: e.bn_aggr(out=mv[:, tt, :], in_=bnst[:, tt, :]),
                     [f"bnst{tt}"], [f"mv{tt}"])
                ACT(lsd[:, tt:tt + 1], mv[:, tt, 1:2], AF.Sqrt, [f"mv{tt}", "epsb"], [f"lsd{tt}"],
                    scale=1.0, bias=epsb)
                S.op("dve", lambda e, tt=tt: e.reciprocal(out=lrs[:, tt:tt + 1], in_=lsd[:, tt:tt + 1]),
                     [f"lsd{tt}"], [f"lrs{tt}"])
                TS(vnt, gvt, mv[:, tt, 0:1], lrs[:, tt:tt + 1], ALU.subtract, ALU.mult,
                   gvr + [f"mv{tt}", f"lrs{tt}"], vnr)
            for gi in range(8):
                ps, pr = nextps()
                for tt in range(4):
                    vnt, vnr = A3.bf16(tt * 2048, 1024)
                    MM(ps[:, tt * 128:(tt + 1) * 128], vnt[:, gi * 128:(gi + 1) * 128], wsbf[:, gi, :],
                       True, True, vnr + ["wsbf"], [pr], inc=(tt == 3))
                tm = tmpm[gi % 2]; tmr = f"tmpm{gi % 2}"
                for tt in range(4):
                    STT(tm[:, tt * 128:(tt + 1) * 128], ps[:, tt * 128:(tt + 1) * 128], pvc(l, 88 + gi),
                        Cg[:, gi, :], ALU.mult, ALU.add, [pr, "pv", "Cg"], [tmr])
                guj, gur = A1.f32(gi * 2048, TG)
                TT(yb_pre[:, gi, :], tm, guj, ALU.mult, [tmr] + gur, [f"yb{gi}"])

            for ab in range(2):
                W, wr = wget([(win[:, 4608 + ab * 512: 4608 + (ab + 1) * 512], 512)], 8)
                for ji in range(4):
                    j = 4 * ab + ji
                    sgj, sgr = A2.f32(j * 2048, TG)
                    ps, pr = nextps()
                    for k in range(8):
                        MM(ps, W(k, ji * 128, 128), h[:, k, :], k == 0, k == 7, [wr, f"h{k}"], [pr])
                    ACT(sgj, ps, AF.Sigmoid, [pr], sgr)
            for ab in range(2):
                W, wr = wget([(wa_d[l][:, ab * 512:(ab + 1) * 512], 512)], 10)
                for ji in range(4):
                    j = 4 * ab + ji
                    sgj, sgr = A2.f32(j * 2048, TG)
                    ps, pr = nextps()
                    for c in range(10):
                        MM(ps, W(c, ji * 128, 128), ya_pre[:, c, :], c == 0, c == 9, [wr, f"ya{c}"], [pr])
                    TT(sgj, ps, sgj, ALU.mult, [pr] + sgr, sgr)
            for ab in range(2):
                W, wr = wget([(win[:, 5632 + ab * 512: 5632 + (ab + 1) * 512], 512)], 8)
                for ji in range(4):
                    j = 4 * ab + ji
                    s2j, s2r = A1.f32(j * 2048, TG)
                    ps, pr = nextps()
                    for k in range(8):
                        MM(ps, W(k, ji * 128, 128), h[:, k, :], k == 0, k == 7, [wr, f"h{k}"], [pr])
                    ACT(s2j, ps, AF.Sigmoid, [pr], s2r)
            for ab in range(2):
                W, wr = wget([(wb_d[l][:, ab * 512:(ab + 1) * 512], 512)], 8)
                for ji in range(4):
                    j = 4 * ab + ji
                    sgj, sgr = A2.f32(j * 2048, TG)
                    s2j, s2r = A1.f32(j * 2048, TG)
                    mj, mr = A3.bf16(j * 1024, TG)
                    ps, pr = nextps()
                    for c in range(8):
                        MM(ps, W(c, ji * 128, 128), yb_pre[:, c, :], c == 0, c == 7, [wr, f"yb{c}"], [pr])
                    TT(s2j, ps, s2j, ALU.mult, [pr] + s2r, s2r)
                    TT(mj, sgj, s2j, ALU.add, sgr + s2r, mr)
            for ob in range(2):
                W, wr = wget([(wo_d[l][:, ob * 512:(ob + 1) * 512], 512)], 8)
                for ji in range(4):
                    j = 4 * ob + ji
                    ps, pr = nextps()
                    for k in range(8):
                        mk, mkr = A3.bf16(k * 1024, TG)
                        MM(ps, W(k, ji * 128, 128), mk, k == 0, k == 7, [wr] + mkr, [pr])
                    TT(xg[:, j, :], xg[:, j, :], ps, ALU.add, [pr, f"xg{j}"], [f"xg{j}"])

            rmsnorm(l, 104)
            for ub in range(8):
                W, wr = wget([(wu_d[l][:, ub * 512:(ub + 1) * 512], 512)], 8)
                for ji in range(4):
                    jf = 4 * ub + ji
                    fj, fr = FR.bf16(jf * 1024, TG)
                    ps, pr = nextps()
                    for k in range(8):
                        MM(ps, W(k, ji * 128, 128), h[:, k, :], k == 0, k == 7, [wr, f"h{k}"], [pr])
                    rr = rl[jf % 2]; rrr = f"rl{jf % 2}"
                    ACT(rr, ps, AF.Relu, [pr], [rrr])
                    TT(fj, rr, rr, ALU.mult, [rrr], fr)
            for j in range(8):
                W, wr = wget([(wd_d[l][:, j * 128:(j + 1) * 128], 128)], 32)
                ps, pr = nextps()
                for kf in range(32):
                    fk, fkr = FR.bf16(kf * 1024, TG)
                    MM(ps, W(kf, 0, 128), fk, kf == 0, kf == 31, [wr] + fkr, [pr])
                TT(xg[:, j, :], xg[:, j, :], ps, ALU.add, [pr, f"xg{j}"], [f"xg{j}"])

            if l < NL - 1:
                dst = xs_d.rearrange("(k p) t -> p k t", p=128)[:, :, cols]
                S.dma("sp", "st_x", lambda e, d=dst: e.dma_start(out=d, in_=xg), xres, [f"xs{g}"])
            else:
                rmsnorm(l, None)
                yres = []
                for k in range(8):
                    yk, ykr = A1.f32(k * 2048, TG)
                    STT(yk, xg[:, k, :], pvc(l, 112 + k), rstd, ALU.mult, ALU.mult,
                        [f"xg{k}", "pv", "rstd"], ykr)
                    yres += ykr
                yv = A1.t32.rearrange("p (k t) -> p k t", k=8)
                dst = yT_d.rearrange("(k p) t -> p k t", p=128)[:, :, cols]
                S.dma("sp", "st_y", lambda e, d=dst, yv=yv: e.dma_start(out=d, in_=yv), yres, [f"y{g}"])

    if dry:
        return recorded
    S.finish(["st_y", "st_x"])
    S.emit()
    return nc


def _pack_params(p):
    f = np.float32
    pv = np.zeros((NL, 128, NPV), f)
    gw = np.zeros((NL, 128, 2, 10, 128), f)
    for l in range(NL):
        def put(col0, vec):
            n = vec.shape[0] // 128
            pv[l, :, col0:col0 + n] = vec.reshape(n, 128).T
        put(0, p["norm_mix_g"][l])
        for tap in range(4):
            put(8 + tap * 10, p["conv_w"][l, tap])
        put(48, p["conv_b"][l])
        put(58, p["lru_b_a"][l].reshape(-1))
        put(68, p["lru_b_x"][l].reshape(-1))
        put(78, p["lru_lambda"][l])
        put(88, p["sgu_ln_g"][l])
        put(96, p["sgu_ln_b"][l])
        put(104, p["norm_ffn_g"][l])
        put(112, p["final_norm_g"])
        for hd in range(20):
            c, hh = divmod(hd, 2)
            gw[l, hh * 64:(hh + 1) * 64, 0, c, hh * 64:(hh + 1) * 64] = p["lru_w_a"][l, hd]
            gw[l, hh * 64:(hh + 1) * 64, 1, c, hh * 64:(hh + 1) * 64] = p["lru_w_x"][l, hd]
    wsT = np.ascontiguousarray(np.transpose(p["sgu_w_s"], (0, 3, 1, 2))).reshape(NL, 128, 8 * 128)
    bsb = np.ascontiguousarray(np.broadcast_to(p["sgu_b_s"].reshape(NL, 1, 8 * 128), (NL, 128, 8 * 128)))
    return pv, gw.reshape(NL, 128, 2 * 10 * 128), wsT.astype(f), bsb.astype(f)


_CACHE = {}


def _program(T):
    if T not in _CACHE:
        plan = build(T, None)
        _CACHE[T] = build(T, plan)
    return _CACHE[T]


def kernel(**inputs):
    p = {k: np.asarray(v, dtype=np.float32) for k, v in inputs.items()}
    x = p["x"]
    B, SEQ, _ = x.shape
    pv, gw, wsT, bsb = _pack_params(p)
    n = 8
    T = SEQ
    nc = _program(T)
    shared = {
        "w_in": p["w_in"], "w_branch_a": p["w_branch_a"], "w_branch_b": p["w_branch_b"],
        "w_out": p["w_out"], "w_up": p["w_up"], "w_down": p["w_down"],
        "pv": pv, "gw": gw, "wsT": wsT, "bsb": bsb,
    }
    in_maps = []
    for c in range(n):
        b = c % B
        m = dict(shared)
        m["xT"] = np.ascontiguousarray(x[b].T)
        in_maps.append(m)
    res = run_bass_kernel_spmd(nc, in_maps, core_ids=list(range(n)))
    out = np.empty((B, SEQ, D), np.float32)
    for b in range(B):
        out[b] = res.results[b]["yT"].T
    return out
```

```python
import numpy as np
import concourse.bass as bass
import concourse.mybir as mybir
from concourse.bass_utils import run_bass_kernel_spmd

F32 = mybir.dt.float32
BF16 = mybir.dt.bfloat16
AF = mybir.ActivationFunctionType
ALU = mybir.AluOpType

D = 1024
DR = 1280
DS = 1024
DFF = 4096
DIN = 6656
NL = 2
EPS = 1e-6
TG = 512
NPV = 120
ENGS = ("pe", "act", "dve", "pool", "sp")


class Sched:
    def __init__(self, nc):
        self.nc = nc
        self.q = {e: [] for e in ENGS}
        self.sems = {}
        self.cnt = {}
        self.seen = {e: {} for e in ENGS}
        self.last_w = {}
        self.readers = {}

    def sem(self, key):
        if key not in self.sems:
            self.sems[key] = self.nc.alloc_semaphore(name="s_" + key)
            self.cnt[key] = 0
        return self.sems[key]

    def _deps(self, engine, reads, writes):
        best = {}
        def add(ev):
            sk, v = ev
            if engine == "pe" and sk == "pe":
                return
            if best.get(sk, 0) < v:
                best[sk] = v
        for r in reads:
            ev = self.last_w.get(r)
            if ev is not None:
                add(ev)
        for w in writes:
            ev = self.last_w.get(w)
            if ev is not None:
                add(ev)
            for ev in self.readers.get(w, ()):
                add(ev)
        waits = []
        for sk, v in best.items():
            if self.seen[engine].get(sk, 0) < v:
                self.seen[engine][sk] = v
                waits.append((sk, v))
        return waits

    def _record(self, ev, reads, writes):
        for r in reads:
            self.readers.setdefault(r, []).append(ev)
        for w in writes:
            self.last_w[w] = ev
            self.readers[w] = []

    def op(self, engine, fn, reads=(), writes=(), inc=True):
        self.sem(engine)
        waits = self._deps(engine, reads, writes)
        ev = (engine, self.cnt[engine] + 1)
        if inc:
            self.cnt[engine] += 1
        else:
            assert engine == "pe"
        self._record(ev, reads, writes)
        self.q[engine].append((waits, fn, (engine, 1) if inc else None))

    def dma(self, queue, semkey, fn, reads=(), writes=()):
        self.sem(semkey)
        waits = self._deps(queue, reads, writes)
        self.cnt[semkey] += 16
        ev = (semkey, self.cnt[semkey])
        self._record(ev, reads, writes)
        self.q[queue].append((waits, fn, (semkey, 16)))

    def finish(self, dma_semkeys):
        waits = [(sk, self.cnt[sk]) for sk in dma_semkeys if sk in self.cnt]
        self.q["sp"].append((waits, None, None))

    def emit(self):
        nc = self.nc
        sems = self.sems

        def run(eng, items):
            for waits, fn, inc in items:
                for sk, v in waits:
                    eng.wait_ge(sems[sk], v)
                if fn is None:
                    continue
                ins = fn(eng)
                if inc is not None:
                    ins.then_inc(sems[inc[0]], inc[1])

        with nc.Block() as block:
            @block.sync
            def _(e):
                run(e, self.q["sp"])

            @block.tensor
            def _(e):
                run(e, self.q["pe"])

            @block.scalar
            def _(e):
                run(e, self.q["act"])

            @block.vector
            def _(e):
                run(e, self.q["dve"])

            @block.gpsimd
            def _(e):
                run(e, self.q["pool"])


class Region:
    def __init__(self, nc, name, nbytes):
        self.name = name
        self.t32 = nc.alloc_sbuf_tensor("rg_" + name, [128, nbytes // 4], F32).ap()
        self.t16 = self.t32.bitcast(BF16)

    def _res(self, off, nb):
        return [f"{self.name}.{u}" for u in range(off // 1024, (off + nb + 1023) // 1024)]

    def f32(self, off, n):
        return self.t32[:, off // 4: off // 4 + n], self._res(off, n * 4)

    def bf16(self, off, n):
        return self.t16[:, off // 2: off // 2 + n], self._res(off, n * 2)


def build(T, plan=None):
    dry = plan is None
    NG = T // TG
    nc = bass.Bass("TRN2", target_bir_lowering=False)
    S = Sched(nc)

    def din(name, shape):
        return nc.dram_tensor(name, shape, F32, kind="ExternalInput").ap()

    xT_d = din("xT", [D, T])
    win_d = din("w_in", [NL, D, DIN])
    wa_d = din("w_branch_a", [NL, DR, D])
    wb_d = din("w_branch_b", [NL, DS, D])
    wo_d = din("w_out", [NL, D, D])
    wu_d = din("w_up", [NL, D, DFF])
    wd_d = din("w_down", [NL, DFF, D])
    pv_d = din("pv", [NL, 128, NPV])
    gw_d = din("gw", [NL, 128, 2 * 10 * 128])
    ws_d = din("wsT", [NL, 128, 8 * 128])
    bsb_d = din("bsb", [NL, 128, 8 * 128])
    xh_d = din("xh", [D, 3])
    oh_d = din("oh", [2, 1])
    sel_d = din("sel", [2, 1])
    yT_d = nc.dram_tensor("yT", [D, T], F32, kind="ExternalOutput").ap()
    xs_d = nc.dram_tensor("xs", [D, T], F32, kind="Internal").ap()
    y1_d = nc.dram_tensor("y1s", [DR, T], BF16, kind="Internal").ap()
    y2_d = nc.dram_tensor("y2s", [DR, T], BF16, kind="Internal").ap()

    def sb(name, shape, dt=F32):
        return nc.alloc_sbuf_tensor("sb_" + name, shape, dt).ap()

    ones32 = sb("ones32", [128, 128])
    onesbf = sb("onesbf", [128, 128], BF16)
    epsb = sb("epsb", [128, 1])
    pv = sb("pv", [128, NL, NPV])
    sct = sb("sct", [128, NL, 10])
    sc = sb("sc", [128, NL, 10])
    sc2 = sb("sc2", [128, NL, 10])
    hsc = sb("hsc", [128, NL, 10])
    hba = sb("hba", [128, NL, 10])
    hbx = sb("hbx", [128, NL, 10])
    gwbf = sb("gwbf", [128, 2, 10, 128], BF16)
    wsbf = sb("wsbf", [128, 8, 128], BF16)
    Cg = sb("Cg", [128, 8, 128])
    halo = sb("halo", [128, 10, 3])
    hstate = sb("hstate", [128, 10])
    pstate = sb("pstate", [128, 10])
    hin = sb("hin", [128, 10])
    xh3 = sb("xh3", [128, 8, 3])
    sq3 = sb("sq3", [128, 8, 3])
    rs3 = sb("rs3", [128, 3])
    hh = sb("hh", [128, 8, 3], BF16)
    zer = sb("zer", [128, TG])
    oh = sb("oh", [2, 1])
    sel = sb("sel", [2, 1])
    y2g = sb("y2g", [128, 10, TG], BF16)
    xgs = [sb(f"xg{i}", [128, 8, TG]) for i in range(2)]
    cur = {"slot": 0, "n": 0}
    h = sb("h", [128, 8, TG], BF16)
    ya_pre = sb("ya_pre", [128, 10, TG], BF16)
    yb_pre = sb("yb_pre", [128, 8, TG], BF16)
    rstd = sb("rstd", [128, TG])
    bnst = sb("bnst", [128, 4, 12])
    mv = sb("mv", [128, 4, 2])
    lsd = sb("lsd", [128, 4])
    lrs = sb("lrs", [128, 4])
    A1 = Region(nc, "A1", 16384)
    A2 = Region(nc, "A2", 16384)
    A3 = Region(nc, "A3", 8192)
    FR = Region(nc, "FR", 32768)
    bsb_v, bsb_r = A2.f32(0, 1024)
    bsb = bsb_v.rearrange("p (g t) -> p g t", g=8)
    acc = [A2.f32(i * 2048, TG) for i in range(2)] + [A2.f32(12288 + i * 2048, TG) for i in range(2)]
    xr_pad = [A2.f32(4096 + i * 3072, TG + 3) for i in range(2)]
    xr_bf = [A2.bf16(10240 + i * 1024, TG) for i in range(2)]
    tmpm = [A2.f32(i * 2048, TG) for i in range(2)]
    rl = [A2.f32(4096 + i * 2048, TG) for i in range(2)]
    WCAP = 5120
    NWB = 4
    wst = [sb(f"wst{i}", [128, WCAP], BF16) for i in range(NWB)]
    pss = [nc.alloc_psum_tensor(f"ps{i}", [128, TG], F32).ap() for i in range(8)]
    psn = [0]

    def nextps():
        i = psn[0] % 8
        psn[0] += 1
        return pss[i], f"ps{i}"

    def MM(out, lhsT, rhs, start, stop, reads, writes, inc=None):
        if inc is None:
            inc = stop
        S.op("pe", lambda e: e.matmul(out, lhsT=lhsT, rhs=rhs, start=start, stop=stop),
             reads, writes, inc=inc)

    def ACT(out, in_, func, reads, writes, **kw):
        S.op("act", lambda e: e.activation(out=out, in_=in_, func=func, **kw), reads, writes)

    def TT(out, in0, in1, op, reads, writes):
        S.op("dve", lambda e: e.tensor_tensor(out=out, in0=in0, in1=in1, op=op), reads, writes)

    def TS(out, in0, s1, s2, op0, op1, reads, writes):
        S.op("dve", lambda e: e.tensor_scalar(out=out, in0=in0, scalar1=s1, scalar2=s2, op0=op0, op1=op1),
             reads, writes)

    def STT(out, in0, scalar, in1, op0, op1, reads, writes):
        S.op("dve", lambda e: e.scalar_tensor_tensor(out=out, in0=in0, scalar=scalar, in1=in1, op0=op0, op1=op1),
             reads, writes)

    def CP(out, in_, reads, writes):
        S.op("dve", lambda e: e.tensor_copy(out=out, in_=in_), reads, writes)

    def MSET(ap, val, writes):
        S.op("dve", lambda e: e.memset(ap, val), (), writes)

    recorded = []
    wstate = {"n": 0, "issued": 0}

    def issue(n):
        pieces, kt, ncols = plan[n]
        bi = n % NWB
        view = wst[bi][:, 0:kt * ncols].rearrange("p (k c) -> p k c", k=kt)
        c0 = 0
        for (src2d, nc_) in pieces:
            dst = view[:, :, c0:c0 + nc_]
            src = src2d.rearrange("(k p) c -> p k c", p=128)
            S.dma("pool", f"w{bi}", lambda e, d=dst, s=src: e.dma_start(out=d, in_=s), (), [f"wst{bi}"])
            c0 += nc_

    def wget(pieces, kt):
        ncols = sum(p[1] for p in pieces)
        assert kt * ncols <= WCAP
        n = wstate["n"]
        wstate["n"] += 1
        if dry:
            recorded.append((pieces, kt, ncols))
            bi = n % NWB
        else:
            while wstate["issued"] < min(n + 3, len(plan)):
                issue(wstate["issued"])
                wstate["issued"] += 1
            bi = n % NWB
        view = wst[bi][:, 0:kt * ncols].rearrange("p (k c) -> p k c", k=kt)

        def W(k, c0, n_):
            return view[:, k, c0:c0 + n_]
        return W, f"wst{bi}"

    def pvc(l, col):
        return pv[:, l, col:col + 1]

    MSET(ones32, 1.0, ["ones32"])
    MSET(onesbf, 1.0, ["onesbf"])
    MSET(epsb, EPS, ["epsb"])
    MSET(zer, 0.0, ["zer"])
    S.dma("sp", "ld_misc", lambda e: e.dma_start(out=oh, in_=oh_d), (), ["oh"])
    S.dma("sp", "ld_sel", lambda e: e.dma_start(out=sel, in_=sel_d), (), ["sel"])
    S.dma("sp", "ld_xh", lambda e: e.dma_start(out=xh3, in_=xh_d.rearrange("(k p) t -> p k t", p=128)),
          (), ["xh3"])
    nex = [0]

    def exchange(src, srcres, f, dst, dstres):
        i = nex[0]
        nex[0] += 1
        n = 128 * f
        scr = nc.dram_tensor(f"exs{i}", [1, n], F32, kind="Internal").ap()
        cin = nc.dram_tensor(f"exi{i}", [2, n], F32, kind="Internal").ap()
        cout = nc.dram_tensor(f"exo{i}", [2, n], F32, kind="Internal").ap()
        G, gr = FR.t32[0:2, 0:n], FR._res(0, n * 4)
        S.dma("sp", "ex", lambda e: e.dma_start(out=scr.rearrange("o (p f) -> (o p) f", p=128), in_=src),
              srcres, [f"exs{i}"])
        S.dma("sp", "ex", lambda e: e.dma_start(out=G, in_=scr.broadcast_to([2, n])), [f"exs{i}"], gr)
        S.op("dve", lambda e: e.tensor_scalar_mul(out=G, in0=G, scalar1=oh[0:2, 0:1]), gr + ["oh"], gr)
        S.dma("sp", "ex", lambda e: e.dma_start(out=cin, in_=G), gr, [f"exi{i}"])
        S.sem("cc")
        waits = S._deps("pool", [f"exi{i}"], [f"exo{i}"])
        for wk in [k_ for k_ in S.cnt if k_ not in ENGS and k_ != "cc"]:
            if S.seen["pool"].get(wk, 0) < S.cnt[wk]:
                S.seen["pool"][wk] = S.cnt[wk]
                waits.append((wk, S.cnt[wk]))
        S.cnt["cc"] += 1
        S._record(("cc", S.cnt["cc"]), [f"exi{i}"], [f"exo{i}"])
        S.q["pool"].append((waits, lambda e: e.collective_compute(
            "AllReduce", ALU.add, replica_groups=[[0, 1], [2, 3], [4, 5], [6, 7]],
            ins=[cin.opt()], outs=[cout.opt()]),
            ("cc", 1)))
        S.seen["pool"]["cc"] = S.cnt["cc"]
        S.q["pool"].append(([("cc", S.cnt["cc"])], None, None))
        S.dma("sp", "ex", lambda e: e.dma_start(out=G, in_=cout), [f"exo{i}"], gr)
        Gv = G.rearrange("r (p f) -> r p f", p=128)
        ps, pr = nextps()
        for j in range(f):
            MM(ps[:, j:j + 1], Gv[:, :, j], sel[0:2, 0:1], True, True, gr + ["sel"], [pr], inc=(j == f - 1))
        CP(dst, ps[:, 0:f], [pr], dstres)
    S.dma("sp", "ld_pv", lambda e: e.dma_start(out=pv, in_=pv_d.rearrange("l p n -> p l n")), (), ["pv"])
    ACT(sct, pv[:, :, 78:88], AF.Exp, ["pv"], ["sct"], scale=-1.0)
    ACT(sct, sct, AF.Ln, ["sct"], ["sct"], scale=1.0, bias=1.0)
    S.op("dve", lambda e: e.tensor_scalar_mul(out=sc, in0=sct, scalar1=-8.0), ["sct"], ["sc"])
    S.op("dve", lambda e: e.tensor_scalar_mul(out=sc2, in0=sct, scalar1=-16.0), ["sct"], ["sc2"])
    S.op("dve", lambda e: e.tensor_scalar_mul(out=hsc, in0=sct, scalar1=-4.0), ["sct"], ["hsc"])
    S.op("dve", lambda e: e.tensor_scalar_mul(out=hba, in0=pv[:, :, 58:68], scalar1=0.5), ["pv"], ["hba"])
    S.op("dve", lambda e: e.tensor_scalar_mul(out=hbx, in0=pv[:, :, 68:78], scalar1=0.5), ["pv"], ["hbx"])

    def rmsnorm(l, gbase, slot):
        xg = xgs[slot]
        for k in range(8):
            sqk, r = A1.f32(k * 2048, TG)
            ACT(sqk, xg[:, k, :], AF.Square, [f"xg{slot}_{k}"], r)
        ps, pr = nextps()
        for k in range(8):
            sqk, r = A1.f32(k * 2048, TG)
            MM(ps, ones32, sqk, k == 0, k == 7, ["ones32"] + r, [pr])
        ACT(rstd, ps, AF.Sqrt, [pr, "epsb"], ["rstd"], scale=1.0 / D, bias=epsb)
        S.op("dve", lambda e: e.reciprocal(out=rstd, in_=rstd), ["rstd"], ["rstd"])
        if gbase is not None:
            for k in range(8):
                STT(h[:, k, :], xg[:, k, :], pvc(l, gbase + k), rstd, ALU.mult, ALU.mult,
                    [f"xg{slot}_{k}", "pv", "rstd"], [f"h{k}"])

    hres = [f"h{k}" for k in range(8)]

    def load_x(l, g):
        cur["n"] += 1
        slot = cur["n"] % 2
        cols = slice(g * TG, (g + 1) * TG)
        src = (xT_d if l == 0 else xs_d).rearrange("(k p) t -> p k t", p=128)[:, :, cols]
        S.dma("sp", f"ld_x{slot}", lambda e, s=src, slot=slot: e.dma_start(out=xgs[slot], in_=s),
              ([] if l == 0 else [f"xs{g}"]), xres(slot))
        return slot

    def xres(slot):
        return [f"xg{slot}_{k}" for k in range(8)]

    def prep2(l, g):
        slot = load_x(l, g)
        rmsnorm(l, 0, slot)
        cols = slice(g * TG, (g + 1) * TG)
        s1 = y1_d.rearrange("(c p) t -> p c t", p=128)[:, :, cols]
        s2 = y2_d.rearrange("(c p) t -> p c t", p=128)[:, :, cols]
        S.dma("sp", "ld_y1", lambda e, s_=s1: e.dma_start(out=ya_pre, in_=s_), [f"y1d{g}"], yares)
        S.dma("sp", "ld_y2", lambda e, s_=s2: e.dma_start(out=y2g, in_=s_), [f"y2d{g}"], y2res)
        for c in range(10):
            STT(ya_pre[:, c, :], y2g[:, c, :], hin[:, c:c + 1], ya_pre[:, c, :], ALU.mult, ALU.add,
                [f"y2{c}", "hin", f"ya{c}"], [f"ya{c}"])
        return slot
    yares = [f"ya{c}" for c in range(10)]
    y2res = [f"y2{c}" for c in range(10)]

    for l in range(NL):
        S.dma("pool", "ld_gw", lambda e, l=l: e.dma_start(
            out=gwbf.rearrange("p a c m -> p (a c m)"), in_=gw_d[l]), (), ["gwbf"])
        S.dma("pool", "ld_ws", lambda e, l=l: e.dma_start(
            out=wsbf.rearrange("p g t -> p (g t)"), in_=ws_d[l]), (), ["wsbf"])
        S.dma("sp", "ld_bsb", lambda e, l=l: e.dma_start(out=bsb_v, in_=bsb_d[l]), (), bsb_r)
        MSET(wsbf[64:128, :, 0:64], 0.0, ["wsbf"])
        for g in range(8):
            ps, pr = nextps()
            MM(ps[:, 0:128], onesbf, wsbf[:, g, :], True, True, ["onesbf", "wsbf"], [pr])
            STT(Cg[:, g, :], ps[:, 0:128], pvc(l, 96 + g), bsb[:, g, :], ALU.mult, ALU.add,
                [pr, "pv"] + bsb_r, ["Cg"])
        MSET(hstate, 0.0, ["hstate"])
        MSET(pstate, 1.0, ["pstate"])
        win = win_d[l]

        S.op("act", lambda e: e.activation(out=sq3, in_=xh3, func=AF.Square), ["xh3"], ["sq3"])
        ps, pr = nextps()
        for k in range(8):
            MM(ps[:, 0:3], ones32, sq3[:, k, :], k == 0, k == 7, ["ones32", "sq3"], [pr])
        ACT(rs3, ps[:, 0:3], AF.Sqrt, [pr, "epsb"], ["rs3"], scale=1.0 / D, bias=epsb)
        S.op("dve", lambda e: e.reciprocal(out=rs3, in_=rs3), ["rs3"], ["rs3"])
        for k in range(8):
            STT(hh[:, k, :], xh3[:, k, :], pvc(l, k), rs3, ALU.mult, ALU.mult, ["xh3", "pv", "rs3"], ["hh"])

        for cp_ in range(5):
            W, wr = wget([(win[:, cp_ * 256: cp_ * 256 + 256], 256)], 8)
            for ci in range(2):
                c = 2 * cp_ + ci
                psh, phr = nextps()
                for k in range(8):
                    MM(psh[:, 0:3], W(k, ci * 128, 128), hh[:, k, :], k == 0, k == 7, [wr, "hh"], [phr])
                CP(halo[:, c, :], psh[:, 0:3], [phr], [f"halo{c}"])

        for g in range(NG):
            cols = slice(g * TG, (g + 1) * TG)
            slot = load_x(l, g)
            rmsnorm(l, 0, slot)
            st = {}

            def bufs(c):
                par = c % 2
                def tmp(idx, par=par):
                    return FR.f32((par * 8 + idx) * 2048, TG)
                return dict(r=tmp(0), i=tmp(1), a=tmp(2), a2=tmp(3), nrm=tmp(4), u=tmp(5), hs=tmp(6), gg=tmp(7),
                            xp=xr_pad[par], ac=acc[c % 4], xb=xr_bf[par])

            def S1(c):
                cp_, ci = divmod(c, 2)
                if ci == 0:
                    st[("W", cp_)] = wget([(win[:, cp_ * 256: cp_ * 256 + 256], 256),
                                           (win[:, DR + cp_ * 256: DR + cp_ * 256 + 256], 256)], 8)
                W, wr = st[("W", cp_)]
                ps1, p1r = pss[c % 2], f"ps{c % 2}"
                for k in range(8):
                    MM(ps1, W(k, ci * 128, 128), h[:, k, :], k == 0, k == 7, [wr, f"h{k}"], [p1r])
                st[("ps1", c)] = (ps1, p1r)

            def S2a(c):
                b = bufs(c)
                xp, xpr = b["xp"]; ac, acr = b["ac"]
                ps1, p1r = st[("ps1", c)]
                ACT(xp[:, 3:TG + 3], ps1, AF.Copy, [p1r], xpr)
                CP(xp[:, 0:3], halo[:, c, :], [f"halo{c}"], xpr)
                TS(ac, xp[:, 3:TG + 3], pvc(l, 8 + 30 + c), pvc(l, 48 + c), ALU.mult, ALU.add,
                   xpr + ["pv"], acr)
                for tap in (2, 1, 0):
                    STT(ac, xp[:, tap:tap + TG], pvc(l, 8 + tap * 10 + c), ac, ALU.mult, ALU.add,
                        xpr + ["pv"] + acr, acr)
                CP(halo[:, c, :], xp[:, TG:TG + 3], xpr, [f"halo{c}"])

            def S2b(c):
                b = bufs(c)
                ac, acr = b["ac"]; xb, xbr = b["xb"]
                ACT(xb, ac, AF.Copy, acr, xbr)

            def S3(c):
                cp_, ci = divmod(c, 2)
                W, wr = st[("W", cp_)]
                b = bufs(c)
                xb, xbr = b["xb"]
                psa, par_ = pss[2 + c % 2], f"ps{2 + c % 2}"
                MM(psa, gwbf[:, 0, c, :], xb, True, True, ["gwbf"] + xbr, [par_])
                psx, pxr = pss[4 + c % 2], f"ps{4 + c % 2}"
                MM(psx, gwbf[:, 1, c, :], xb, True, True, ["gwbf"] + xbr, [pxr])
                psg, pgr = pss[6 + c % 2], f"ps{6 + c % 2}"
                for k in range(8):
                    MM(psg, W(k, 256 + ci * 128, 128), h[:, k, :], k == 0, k == 7, [wr, f"h{k}"], [pgr])
                st[("g", c)] = (psa, par_, psx, pxr, psg, pgr)

            def S4a_te(c):
                b = bufs(c)
                psa, par_, psx, pxr, psg, pgr = st[("g", c)]
                r_, r_r = b["r"]; i_, i_r = b["i"]; a_, a_r = b["a"]; a2_, a2_r = b["a2"]
                ACT(r_, psa, AF.Tanh, [par_, "hba"], r_r, scale=0.5, bias=hba[:, l, c:c + 1])
                ACT(i_, psx, AF.Tanh, [pxr, "hbx"], i_r, scale=0.5, bias=hbx[:, l, c:c + 1])
                ACT(a_, r_, AF.Exp, r_r + ["hsc"], a_r, scale=hsc[:, l, c:c + 1], bias=hsc[:, l, c:c + 1])
                ACT(a2_, r_, AF.Exp, r_r + ["sc"], a2_r, scale=sc[:, l, c:c + 1], bias=sc[:, l, c:c + 1])

            def S4a_s(c):
                b = bufs(c)
                a2_, a2_r = b["a2"]; nrm, nrm_r = b["nrm"]
                ACT(nrm, a2_, AF.Sqrt, a2_r, nrm_r, scale=-0.25, bias=0.25)

            def S4a_g(c):
                b = bufs(c)
                psa, par_, psx, pxr, psg, pgr = st[("g", c)]
                gg, gg_r = b["gg"]
                ACT(gg, psg, AF.Gelu_apprx_tanh, [pgr], gg_r)

            def S4a_pair(c0):
                for fn in (S4a_te, S4a_s, S4a_g):
                    fn(c0)
                    fn(c0 + 1)

            def S4b(c):
                b = bufs(c)
                i_, i_r = b["i"]; a_, a_r = b["a"]; a2_, a2_r = b["a2"]
                nrm, nrm_r = b["nrm"]; u_, u_r = b["u"]; hs, hs_r = b["hs"]; gg, gg_r = b["gg"]
                ac, acr = b["ac"]
                STT(u_, i_, 1.0, nrm, ALU.add, ALU.mult, nrm_r + i_r, u_r)
                TT(u_, u_, ac, ALU.mult, u_r + acr, u_r)
                S.op("dve", lambda e: e.tensor_tensor_scan(
                    out=hs, data0=a_, data1=u_, initial=hstate[:, c:c + 1], op0=ALU.mult, op1=ALU.add),
                    a_r + u_r + ["hstate"], hs_r)
                CP(hstate[:, c:c + 1], hs[:, TG - 1:TG], hs_r, ["hstate"])
                pc, pc_r = a2_, a2_r
                S.op("dve", lambda e: e.tensor_tensor_scan(
                    out=pc, data0=a_, data1=zer, initial=pstate[:, c:c + 1], op0=ALU.mult, op1=ALU.add),
                    a_r + ["zer", "pstate"], pc_r)
                CP(pstate[:, c:c + 1], pc[:, TG - 1:TG], pc_r, ["pstate"])
                TT(ya_pre[:, c, :], hs, gg, ALU.mult, hs_r + gg_r, [f"ya{c}"])
                TT(y2g[:, c, :], pc, gg, ALU.mult, pc_r + gg_r, [f"y2{c}"])

            S1(0)
            S1(1)
            for p_ in range(5):
                c0, c1 = 2 * p_, 2 * p_ + 1
                S2a(c0)
                S2a(c1)
                if c0 + 2 < 10:
                    S1(c0 + 2)
                    S1(c1 + 2)
                if p_ >= 1:
                    S4a_pair(c0 - 2)
                S2b(c0)
                S2b(c1)
                S3(c0)
                S3(c1)
                if p_ >= 1:
                    S4b(c0 - 2)
                    S4b(c1 - 2)
            S4a_pair(8)
            S4b(8)
            S4b(9)
            d1 = y1_d.rearrange("(c p) t -> p c t", p=128)[:, :, cols]
            d2 = y2_d.rearrange("(c p) t -> p c t", p=128)[:, :, cols]
            S.dma("sp", "st_y1", lambda e, d=d1: e.dma_start(out=d, in_=ya_pre), yares, [f"y1d{g}"])
            S.dma("sp", "st_y2", lambda e, d=d2: e.dma_start(out=d, in_=y2g), y2res, [f"y2d{g}"])

        exchange(hstate, ["hstate"], 10, hin, ["hin"])

        order = [NG - 1] + list(range(NG - 1))
        nslot = prep2(l, order[0])
        for oi, g in enumerate(order):
            cols = slice(g * TG, (g + 1) * TG)
            slot = nslot
            xg = xgs[slot]

            for vb in range(2):
                W, wr = wget([(win[:, 3584 + vb * 512: 3584 + (vb + 1) * 512], 512)], 8)
                for tt in range(4):
                    gvt, gvr = A2.f32(tt * 4096 + vb * 2048, 512)
                    ps, pr = nextps()
                    for k in range(8):
                        MM(ps, h[:, k, tt * 128:(tt + 1) * 128], W(k, 0, 512), k == 0, k == 7,
                           [wr, f"h{k}"], [pr])
                    ACT(gvt, ps, AF.Gelu_apprx_tanh, [pr], gvr)
            for ub in range(2):
                W, wr = wget([(win[:, 2560 + ub * 512: 2560 + (ub + 1) * 512], 512)], 8)
                for ji in range(4):
                    j = 4 * ub + ji
                    guj, gur = A1.f32(j * 2048, TG)
                    ps, pr = nextps()
                    for k in range(8):
                        MM(ps, W(k, ji * 128, 128), h[:, k, :], k == 0, k == 7, [wr, f"h{k}"], [pr])
                    ACT(guj, ps, AF.Gelu_apprx_tanh, [pr], gur)
            for tt in range(4):
                gvt, gvr = A2.f32(tt * 4096, 1024)
                vnt, vnr = A3.bf16(tt * 2048, 1024)
                S.op("dve", lambda e, tt=tt, gvt=gvt: e.bn_stats(out=bnst[:, tt, 0:6], in_=gvt[:, 0:512]),
                     gvr, [f"bnst{tt}"])
                S.op("dve", lambda e, tt=tt, gvt=gvt: e.bn_stats(out=bnst[:, tt, 6:12], in_=gvt[:, 512:1024]),
                     gvr, [f"bnst{tt}"])
                S.op("dve", lambda e, tt=tt: e.bn_aggr(out=mv[:, tt, :], in_=bnst[:, tt, :]),
                     [f"bnst{tt}"], [f"mv{tt}"])
                ACT(lsd[:, tt:tt + 1], mv[:, tt, 1:2], AF.Sqrt, [f"mv{tt}", "epsb"], [f"lsd{tt}"],
                    scale=1.0, bias=epsb)
                S.op("dve", lambda e, tt=tt: e.reciprocal(out=lrs[:, tt:tt + 1], in_=lsd[:, tt:tt + 1]),
                     [f"lsd{tt}"], [f"lrs{tt}"])
                TS(vnt, gvt, mv[:, tt, 0:1], lrs[:, tt:tt + 1], ALU.subtract, ALU.mult,
                   gvr + [f"mv{tt}", f"lrs{tt}"], vnr)
            for gi in range(8):
                ps, pr = nextps()
                for tt in range(4):
                    vnt, vnr = A3.bf16(tt * 2048, 1024)
                    MM(ps[:, tt * 128:(tt + 1) * 128], vnt[:, gi * 128:(gi + 1) * 128], wsbf[:, gi, :],
                       True, True, vnr + ["wsbf"], [pr], inc=(tt == 3))
                tm, tmr = tmpm[gi % 2]
                for tt in range(4):
                    STT(tm[:, tt * 128:(tt + 1) * 128], ps[:, tt * 128:(tt + 1) * 128], pvc(l, 88 + gi),
                        Cg[:, gi, :], ALU.mult, ALU.add, [pr, "pv", "Cg"], tmr)
                guj, gur = A1.f32(gi * 2048, TG)
                TT(yb_pre[:, gi, :], tm, guj, ALU.mult, tmr + gur, [f"yb{gi}"])

            for ab in range(2):
                W, wr = wget([(win[:, 4608 + ab * 512: 4608 + (ab + 1) * 512], 512)], 8)
                for ji in range(4):
                    j = 4 * ab + ji
                    sgj, sgr = A2.f32(j * 2048, TG)
                    ps, pr = nextps()
                    for k in range(8):
                        MM(ps, W(k, ji * 128, 128), h[:, k, :], k == 0, k == 7, [wr, f"h{k}"], [pr])
                    ACT(sgj, ps, AF.Sigmoid, [pr], sgr)
            for ab in range(2):
                W, wr = wget([(wa_d[l][:, ab * 512:(ab + 1) * 512], 512)], 10)
                for ji in range(4):
                    j = 4 * ab + ji
                    sgj, sgr = A2.f32(j * 2048, TG)
                    ps, pr = nextps()
                    for c in range(10):
                        MM(ps, W(c, ji * 128, 128), ya_pre[:, c, :], c == 0, c == 9, [wr, f"ya{c}"], [pr])
                    TT(sgj, ps, sgj, ALU.mult, [pr] + sgr, sgr)
            for ab in range(2):
                W, wr = wget([(win[:, 5632 + ab * 512: 5632 + (ab + 1) * 512], 512)], 8)
                for ji in range(4):
                    j = 4 * ab + ji
                    s2j, s2r = A1.f32(j * 2048, TG)
                    ps, pr = nextps()
                    for k in range(8):
                        MM(ps, W(k, ji * 128, 128), h[:, k, :], k == 0, k == 7, [wr, f"h{k}"], [pr])
                    ACT(s2j, ps, AF.Sigmoid, [pr], s2r)
            for ab in range(2):
                W, wr = wget([(wb_d[l][:, ab * 512:(ab + 1) * 512], 512)], 8)
                for ji in range(4):
                    j = 4 * ab + ji
                    sgj, sgr = A2.f32(j * 2048, TG)
                    s2j, s2r = A1.f32(j * 2048, TG)
                    mj, mr = A3.bf16(j * 1024, TG)
                    ps, pr = nextps()
                    for c in range(8):
                        MM(ps, W(c, ji * 128, 128), yb_pre[:, c, :], c == 0, c == 7, [wr, f"yb{c}"], [pr])
                    TT(s2j, ps, s2j, ALU.mult, [pr] + s2r, s2r)
                    TT(mj, sgj, s2j, ALU.add, sgr + s2r, mr)
            for ob in range(2):
                W, wr = wget([(wo_d[l][:, ob * 512:(ob + 1) * 512], 512)], 8)
                for ji in range(4):
                    j = 4 * ob + ji
                    ps, pr = nextps()
                    for k in range(8):
                        mk, mkr = A3.bf16(k * 1024, TG)
                        MM(ps, W(k, ji * 128, 128), mk, k == 0, k == 7, [wr] + mkr, [pr])
                    TT(xg[:, j, :], xg[:, j, :], ps, ALU.add, [pr, f"xg{slot}_{j}"], [f"xg{slot}_{j}"])

            rmsnorm(l, 104, slot)
            for ub in range(8):
                W, wr = wget([(wu_d[l][:, ub * 512:(ub + 1) * 512], 512)], 8)
                for ji in range(4):
                    jf = 4 * ub + ji
                    fj, fr = FR.bf16(jf * 1024, TG)
                    ps, pr = nextps()
                    for k in range(8):
                        MM(ps, W(k, ji * 128, 128), h[:, k, :], k == 0, k == 7, [wr, f"h{k}"], [pr])
                    rr, rrr = rl[jf % 2]
                    ACT(rr, ps, AF.Relu, [pr], rrr)
                    TT(fj, rr, rr, ALU.mult, rrr, fr)
            if oi + 1 < len(order):
                nslot = prep2(l, order[oi + 1])
            for j in range(8):
                W, wr = wget([(wd_d[l][:, j * 128:(j + 1) * 128], 128)], 32)
                ps, pr = nextps()
                for kf in range(32):
                    fk, fkr = FR.bf16(kf * 1024, TG)
                    MM(ps, W(kf, 0, 128), fk, kf == 0, kf == 31, [wr] + fkr, [pr])
                TT(xg[:, j, :], xg[:, j, :], ps, ALU.add, [pr, f"xg{slot}_{j}"], [f"xg{slot}_{j}"])

            if l < NL - 1:
                dst = xs_d.rearrange("(k p) t -> p k t", p=128)[:, :, cols]
                S.dma("sp", "st_x", lambda e, d=dst, xg=xg: e.dma_start(out=d, in_=xg), xres(slot), [f"xs{g}"])
                if g == NG - 1:
                    CP(sq3, xg[:, :, TG - 3:TG], xres(slot), ["sq3"])
                    exchange(sq3.rearrange("p k t -> p (k t)"), ["sq3"], 24,
                             xh3.rearrange("p k t -> p (k t)"), ["xh3"])
            else:
                rmsnorm(l, None, slot)
                yres = []
                for k in range(8):
                    yk, ykr = A1.f32(k * 2048, TG)
                    STT(yk, xg[:, k, :], pvc(l, 112 + k), rstd, ALU.mult, ALU.mult,
                        [f"xg{slot}_{k}", "pv", "rstd"], ykr)
                    yres += ykr
                yv = A1.t32.rearrange("p (k t) -> p k t", k=8)
                dst = yT_d.rearrange("(k p) t -> p k t", p=128)[:, :, cols]
                S.dma("sp", "st_y", lambda e, d=dst, yv=yv: e.dma_start(out=d, in_=yv), yres, [f"y{g}"])

    if dry:
        return recorded
    S.finish(["st_y", "st_x", "st_y1", "st_y2", "ex"])
    S.emit()
    return nc


def _pack_params(p):
    f = np.float32
    pv = np.zeros((NL, 128, NPV), f)
    gw = np.zeros((NL, 128, 2, 10, 128), f)
    for l in range(NL):
        def put(col0, vec):
            n = vec.shape[0] // 128
            pv[l, :, col0:col0 + n] = vec.reshape(n, 128).T
        put(0, p["norm_mix_g"][l])
        for tap in range(4):
            put(8 + tap * 10, p["conv_w"][l, tap])
        put(48, p["conv_b"][l])
        put(58, p["lru_b_a"][l].reshape(-1))
        put(68, p["lru_b_x"][l].reshape(-1))
        put(78, p["lru_lambda"][l])
        put(88, p["sgu_ln_g"][l])
        put(96, p["sgu_ln_b"][l])
        put(104, p["norm_ffn_g"][l])
        put(112, p["final_norm_g"])
        for hd in range(20):
            c, hh = divmod(hd, 2)
            gw[l, hh * 64:(hh + 1) * 64, 0, c, hh * 64:(hh + 1) * 64] = p["lru_w_a"][l, hd]
            gw[l, hh * 64:(hh + 1) * 64, 1, c, hh * 64:(hh + 1) * 64] = p["lru_w_x"][l, hd]
    wsT = np.ascontiguousarray(np.transpose(p["sgu_w_s"], (0, 3, 1, 2))).reshape(NL, 128, 8 * 128)
    bsb = np.ascontiguousarray(np.broadcast_to(p["sgu_b_s"].reshape(NL, 1, 8 * 128), (NL, 128, 8 * 128)))
    return pv, gw.reshape(NL, 128, 2 * 10 * 128), wsT.astype(f), bsb.astype(f)


_CACHE = {}


def _program(T):
    if T not in _CACHE:
        plan = build(T, None)
        _CACHE[T] = build(T, plan)
    return _CACHE[T]


def kernel(**inputs):
    p = {k: np.asarray(v, dtype=np.float32) for k, v in inputs.items()}
    x = p["x"]
    B, SEQ, _ = x.shape
    pv, gw, wsT, bsb = _pack_params(p)
    n = 8
    halves = n // B
    T = SEQ // halves
    nc = _program(T)
    shared = {
        "w_in": p["w_in"], "w_branch_a": p["w_branch_a"], "w_branch_b": p["w_branch_b"],
        "w_out": p["w_out"], "w_up": p["w_up"], "w_down": p["w_down"],
        "pv": pv, "gw": gw, "wsT": wsT, "bsb": bsb,
    }
    in_maps = []
    for c in range(n):
        b, hf = divmod(c, halves)
        m = dict(shared)
        m["xT"] = np.ascontiguousarray(x[b, hf * T:(hf + 1) * T].T)
        xh = np.zeros((D, 3), np.float32)
        oh = np.zeros((2, 1), np.float32)
        sel = np.zeros((2, 1), np.float32)
        oh[hf, 0] = 1.0
        if hf > 0:
            xh[:] = x[b, hf * T - 3:hf * T].T
            sel[hf - 1, 0] = 1.0
        m["xh"], m["oh"], m["sel"] = xh, oh, sel
        in_maps.append(m)
    res = run_bass_kernel_spmd(nc, in_maps, core_ids=list(range(n)))
    out = np.empty((B, SEQ, D), np.float32)
    for c in range(n):
        b, hf = divmod(c, halves)
        out[b, hf * T:(hf + 1) * T] = res.results[c]["yT"].T
    return out
```

```python
import numpy as np
import concourse.bass as bass
import concourse.mybir as mybir
from concourse.bass_utils import run_bass_kernel_spmd

F32 = mybir.dt.float32
BF16 = mybir.dt.bfloat16
AF = mybir.ActivationFunctionType
ALU = mybir.AluOpType

D = 1024
DR = 1280
DS = 1024
DFF = 4096
DIN = 6656
NL = 2
EPS = 1e-6
TG = 512
NPV = 120
ENGS = ("pe", "act", "dve", "pool", "sp")


class Sched:
    def __init__(self, nc):
        self.nc = nc
        self.q = {e: [] for e in ENGS}
        self.sems = {}
        self.cnt = {}
        self.seen = {e: {} for e in ENGS}
        self.last_w = {}
        self.readers = {}

    def sem(self, key):
        if key not in self.sems:
            self.sems[key] = self.nc.alloc_semaphore(name="s_" + key)
            self.cnt[key] = 0
        return self.sems[key]

    def _deps(self, engine, reads, writes):
        best = {}
        def add(ev):
            sk, v = ev
            if engine == "pe" and sk == "pe":
                return
            if best.get(sk, 0) < v:
                best[sk] = v
        for r in reads:
            ev = self.last_w.get(r)
            if ev is not None:
                add(ev)
        for w in writes:
            ev = self.last_w.get(w)
            if ev is not None:
                add(ev)
            for ev in self.readers.get(w, ()):
                add(ev)
        waits = []
        for sk, v in best.items():
            if self.seen[engine].get(sk, 0) < v:
                self.seen[engine][sk] = v
                waits.append((sk, v))
        return waits

    def _record(self, ev, reads, writes):
        for r in reads:
            self.readers.setdefault(r, []).append(ev)
        for w in writes:
            self.last_w[w] = ev
            self.readers[w] = []

    def op(self, engine, fn, reads=(), writes=(), inc=True):
        self.sem(engine)
        waits = self._deps(engine, reads, writes)
        ev = (engine, self.cnt[engine] + 1)
        if inc:
            self.cnt[engine] += 1
        else:
            assert engine == "pe"
        self._record(ev, reads, writes)
        self.q[engine].append((waits, fn, (engine, 1) if inc else None))

    def dma(self, queue, semkey, fn, reads=(), writes=()):
        self.sem(semkey)
        waits = self._deps(queue, reads, writes)
        self.cnt[semkey] += 16
        ev = (semkey, self.cnt[semkey])
        self._record(ev, reads, writes)
        self.q[queue].append((waits, fn, (semkey, 16)))

    def finish(self, dma_semkeys):
        waits = [(sk, self.cnt[sk]) for sk in dma_semkeys if sk in self.cnt]
        self.q["sp"].append((waits, None, None))

    def emit(self):
        nc = self.nc
        sems = self.sems

        def run(eng, items):
            for waits, fn, inc in items:
                for sk, v in waits:
                    eng.wait_ge(sems[sk], v)
                if fn is None:
                    continue
                ins = fn(eng)
                if inc is not None:
                    ins.then_inc(sems[inc[0]], inc[1])

        with nc.Block() as block:
            @block.sync
            def _(e):
                run(e, self.q["sp"])

            @block.tensor
            def _(e):
                run(e, self.q["pe"])

            @block.scalar
            def _(e):
                run(e, self.q["act"])

            @block.vector
            def _(e):
                run(e, self.q["dve"])

            @block.gpsimd
            def _(e):
                run(e, self.q["pool"])


class Region:
    def __init__(self, nc, name, nbytes):
        self.name = name
        self.t32 = nc.alloc_sbuf_tensor("rg_" + name, [128, nbytes // 4], F32).ap()
        self.t16 = self.t32.bitcast(BF16)

    def _res(self, off, nb):
        return [f"{self.name}.{u}" for u in range(off // 1024, (off + nb + 1023) // 1024)]

    def f32(self, off, n):
        return self.t32[:, off // 4: off // 4 + n], self._res(off, n * 4)

    def bf16(self, off, n):
        return self.t16[:, off // 2: off // 2 + n], self._res(off, n * 2)


def build(T, plan=None):
    dry = plan is None
    NG = T // TG
    nc = bass.Bass("TRN2", target_bir_lowering=False)
    S = Sched(nc)

    def din(name, shape):
        return nc.dram_tensor(name, shape, F32, kind="ExternalInput").ap()

    xT_d = din("xT", [D, T])
    win_d = din("w_in", [NL, D, DIN])
    wa_d = din("w_branch_a", [NL, DR, D])
    wb_d = din("w_branch_b", [NL, DS, D])
    wo_d = din("w_out", [NL, D, D])
    wu_d = din("w_up", [NL, D, DFF])
    wd_d = din("w_down", [NL, DFF, D])
    pv_d = din("pv", [NL, 128, NPV])
    gw_d = din("gw", [NL, 128, 2 * 10 * 128])
    ws_d = din("wsT", [NL, 128, 8 * 128])
    bsb_d = din("bsb", [NL, 128, 8 * 128])
    xh_d = din("xh", [D, 3])
    oh_d = din("oh", [2, 1])
    sel_d = din("sel", [2, 1])
    yT_d = nc.dram_tensor("yT", [D, T], F32, kind="ExternalOutput").ap()
    xs_d = nc.dram_tensor("xs", [D, T], F32, kind="Internal").ap()
    y1_d = nc.dram_tensor("y1s", [DR, T], BF16, kind="Internal").ap()
    y2_d = nc.dram_tensor("y2s", [DR, T], BF16, kind="Internal").ap()

    def sb(name, shape, dt=F32):
        return nc.alloc_sbuf_tensor("sb_" + name, shape, dt).ap()

    ones32 = sb("ones32", [128, 128])
    onesbf = sb("onesbf", [128, 128], BF16)
    epsb = sb("epsb", [128, 1])
    pv = sb("pv", [128, NL, NPV])
    sct = sb("sct", [128, NL, 10])
    sc = sb("sc", [128, NL, 10])
    sc2 = sb("sc2", [128, NL, 10])
    hsc = sb("hsc", [128, NL, 10])
    hba = sb("hba", [128, NL, 10])
    hbx = sb("hbx", [128, NL, 10])
    gwbf = sb("gwbf", [128, 2, 10, 128], BF16)
    wsbf = sb("wsbf", [128, 8, 128], BF16)
    Cg = sb("Cg", [128, 8, 128])
    halo = sb("halo", [128, 10, 3])
    hstate = sb("hstate", [128, 10])
    pstate = sb("pstate", [128, 10])
    hin = sb("hin", [128, 10])
    xh3 = sb("xh3", [128, 8, 3])
    sq3 = sb("sq3", [128, 8, 3])
    rs3 = sb("rs3", [128, 3])
    hh = sb("hh", [128, 8, 3], BF16)
    zer = sb("zer", [128, TG])
    oh = sb("oh", [2, 1])
    sel = sb("sel", [2, 1])
    y2g = sb("y2g", [128, 10, TG], BF16)
    xgs = [sb(f"xg{i}", [128, 8, TG]) for i in range(2)]
    cur = {"slot": 0, "n": 0}
    h = sb("h", [128, 8, TG], BF16)
    ya_pre = sb("ya_pre", [128, 10, TG], BF16)
    yb_pre = sb("yb_pre", [128, 8, TG], BF16)
    rstd = sb("rstd", [128, TG])
    bnst = sb("bnst", [128, 4, 12])
    mv = sb("mv", [128, 4, 2])
    lsd = sb("lsd", [128, 4])
    lrs = sb("lrs", [128, 4])
    A1 = Region(nc, "A1", 16384)
    A2 = Region(nc, "A2", 16384)
    A3 = Region(nc, "A3", 8192)
    FR = Region(nc, "FR", 32768)
    bsb_v, bsb_r = A2.f32(0, 1024)
    bsb = bsb_v.rearrange("p (g t) -> p g t", g=8)
    acc = [A2.f32(i * 2048, TG) for i in range(2)] + [A2.f32(12288 + i * 2048, TG) for i in range(2)]
    xr_pad = [A2.f32(4096 + i * 3072, TG + 3) for i in range(2)]
    xr_bf = [A2.bf16(10240 + i * 1024, TG) for i in range(2)]
    tmpm = [A2.f32(i * 2048, TG) for i in range(2)]
    rl = [A2.f32(4096 + i * 2048, TG) for i in range(2)]
    WCAP = 5120
    NWB = 4
    wst = [sb(f"wst{i}", [128, WCAP], BF16) for i in range(NWB)]
    pss = [nc.alloc_psum_tensor(f"ps{i}", [128, TG], F32).ap() for i in range(8)]
    psn = [0]

    def nextps():
        i = psn[0] % 8
        psn[0] += 1
        return pss[i], f"ps{i}"

    def MM(out, lhsT, rhs, start, stop, reads, writes, inc=None):
        if inc is None:
            inc = stop
        S.op("pe", lambda e: e.matmul(out, lhsT=lhsT, rhs=rhs, start=start, stop=stop),
             reads, writes, inc=inc)

    def ACT(out, in_, func, reads, writes, **kw):
        S.op("act", lambda e: e.activation(out=out, in_=in_, func=func, **kw), reads, writes)

    def TT(out, in0, in1, op, reads, writes):
        S.op("dve", lambda e: e.tensor_tensor(out=out, in0=in0, in1=in1, op=op), reads, writes)

    def TS(out, in0, s1, s2, op0, op1, reads, writes):
        S.op("dve", lambda e: e.tensor_scalar(out=out, in0=in0, scalar1=s1, scalar2=s2, op0=op0, op1=op1),
             reads, writes)

    def STT(out, in0, scalar, in1, op0, op1, reads, writes):
        S.op("dve", lambda e: e.scalar_tensor_tensor(out=out, in0=in0, scalar=scalar, in1=in1, op0=op0, op1=op1),
             reads, writes)

    def CP(out, in_, reads, writes):
        S.op("dve", lambda e: e.tensor_copy(out=out, in_=in_), reads, writes)

    def MSET(ap, val, writes):
        S.op("dve", lambda e: e.memset(ap, val), (), writes)

    recorded = []
    wstate = {"n": 0, "issued": 0}

    def issue(n):
        pieces, kt, ncols = plan[n]
        bi = n % NWB
        view = wst[bi][:, 0:kt * ncols].rearrange("p (k c) -> p k c", k=kt)
        c0 = 0
        for (src2d, nc_) in pieces:
            dst = view[:, :, c0:c0 + nc_]
            src = src2d.rearrange("(k p) c -> p k c", p=128)
            S.dma("pool", f"w{bi}", lambda e, d=dst, s=src: e.dma_start(out=d, in_=s), (), [f"wst{bi}"])
            c0 += nc_

    def wget(pieces, kt):
        ncols = sum(p[1] for p in pieces)
        assert kt * ncols <= WCAP
        n = wstate["n"]
        wstate["n"] += 1
        if dry:
            recorded.append((pieces, kt, ncols))
            bi = n % NWB
        else:
            while wstate["issued"] < min(n + 3, len(plan)):
                issue(wstate["issued"])
                wstate["issued"] += 1
            bi = n % NWB
        view = wst[bi][:, 0:kt * ncols].rearrange("p (k c) -> p k c", k=kt)

        def W(k, c0, n_):
            return view[:, k, c0:c0 + n_]
        return W, f"wst{bi}"

    def pvc(l, col):
        return pv[:, l, col:col + 1]

    MSET(ones32, 1.0, ["ones32"])
    MSET(onesbf, 1.0, ["onesbf"])
    MSET(epsb, EPS, ["epsb"])
    MSET(zer, 0.0, ["zer"])
    S.dma("sp", "ld_misc", lambda e: e.dma_start(out=oh, in_=oh_d), (), ["oh"])
    S.dma("sp", "ld_sel", lambda e: e.dma_start(out=sel, in_=sel_d), (), ["sel"])
    S.dma("sp", "ld_xh", lambda e: e.dma_start(out=xh3, in_=xh_d.rearrange("(k p) t -> p k t", p=128)),
          (), ["xh3"])
    nex = [0]

    def exchange(src, srcres, f, dst, dstres):
        i = nex[0]
        nex[0] += 1
        n = 128 * f
        scr = nc.dram_tensor(f"exs{i}", [1, n], F32, kind="Internal").ap()
        cin = nc.dram_tensor(f"exi{i}", [2, n], F32, kind="Internal").ap()
        cout = nc.dram_tensor(f"exo{i}", [2, n], F32, kind="Internal").ap()
        G, gr = FR.t32[0:2, 0:n], FR._res(0, n * 4)
        S.dma("sp", "ex", lambda e: e.dma_start(out=scr.rearrange("o (p f) -> (o p) f", p=128), in_=src),
              srcres, [f"exs{i}"])
        S.dma("sp", "ex", lambda e: e.dma_start(out=G, in_=scr.broadcast_to([2, n])), [f"exs{i}"], gr)
        S.op("dve", lambda e: e.tensor_scalar_mul(out=G, in0=G, scalar1=oh[0:2, 0:1]), gr + ["oh"], gr)
        S.dma("sp", "ex", lambda e: e.dma_start(out=cin, in_=G), gr, [f"exi{i}"])
        S.sem("cc")
        waits = S._deps("pool", [f"exi{i}"], [f"exo{i}"])
        for wk in [k_ for k_ in S.cnt if k_ not in ENGS and k_ != "cc"]:
            if S.seen["pool"].get(wk, 0) < S.cnt[wk]:
                S.seen["pool"][wk] = S.cnt[wk]
                waits.append((wk, S.cnt[wk]))
        S.cnt["cc"] += 1
        S._record(("cc", S.cnt["cc"]), [f"exi{i}"], [f"exo{i}"])
        S.q["pool"].append((waits, lambda e: e.collective_compute(
            "AllReduce", ALU.add, replica_groups=[[0, 1], [2, 3], [4, 5], [6, 7]],
            ins=[cin.opt()], outs=[cout.opt()]),
            ("cc", 1)))
        S.seen["pool"]["cc"] = S.cnt["cc"]
        S.q["pool"].append(([("cc", S.cnt["cc"])], None, None))
        def recv():
            S.dma("sp", "ex", lambda e: e.dma_start(out=G, in_=cout), [f"exo{i}"], gr)
            Gv = G.rearrange("r (p f) -> r p f", p=128)
            ps, pr = nextps()
            for j in range(f):
                MM(ps[:, j:j + 1], Gv[:, :, j], sel[0:2, 0:1], True, True, gr + ["sel"], [pr], inc=(j == f - 1))
            CP(dst, ps[:, 0:f], [pr], dstres)
        return recv
    S.dma("sp", "ld_pv", lambda e: e.dma_start(out=pv, in_=pv_d.rearrange("l p n -> p l n")), (), ["pv"])
    ACT(sct, pv[:, :, 78:88], AF.Exp, ["pv"], ["sct"], scale=-1.0)
    ACT(sct, sct, AF.Ln, ["sct"], ["sct"], scale=1.0, bias=1.0)
    S.op("dve", lambda e: e.tensor_scalar_mul(out=sc, in0=sct, scalar1=-8.0), ["sct"], ["sc"])
    S.op("dve", lambda e: e.tensor_scalar_mul(out=sc2, in0=sct, scalar1=-16.0), ["sct"], ["sc2"])
    S.op("dve", lambda e: e.tensor_scalar_mul(out=hsc, in0=sct, scalar1=-4.0), ["sct"], ["hsc"])
    S.op("dve", lambda e: e.tensor_scalar_mul(out=hba, in0=pv[:, :, 58:68], scalar1=0.5), ["pv"], ["hba"])
    S.op("dve", lambda e: e.tensor_scalar_mul(out=hbx, in0=pv[:, :, 68:78], scalar1=0.5), ["pv"], ["hbx"])

    def rmsnorm(l, gbase, slot):
        xg = xgs[slot]
        for k in range(8):
            sqk, r = A1.f32(k * 2048, TG)
            ACT(sqk, xg[:, k, :], AF.Square, [f"xg{slot}_{k}"], r)
        ps, pr = nextps()
        for k in range(8):
            sqk, r = A1.f32(k * 2048, TG)
            MM(ps, ones32, sqk, k == 0, k == 7, ["ones32"] + r, [pr])
        ACT(rstd, ps, AF.Sqrt, [pr, "epsb"], ["rstd"], scale=1.0 / D, bias=epsb)
        S.op("dve", lambda e: e.reciprocal(out=rstd, in_=rstd), ["rstd"], ["rstd"])
        if gbase is not None:
            for k in range(8):
                STT(h[:, k, :], xg[:, k, :], pvc(l, gbase + k), rstd, ALU.mult, ALU.mult,
                    [f"xg{slot}_{k}", "pv", "rstd"], [f"h{k}"])

    hres = [f"h{k}" for k in range(8)]

    def load_x(l, g):
        cur["n"] += 1
        slot = cur["n"] % 2
        cols = slice(g * TG, (g + 1) * TG)
        src = (xT_d if l == 0 else xs_d).rearrange("(k p) t -> p k t", p=128)[:, :, cols]
        S.dma("sp", f"ld_x{slot}", lambda e, s=src, slot=slot: e.dma_start(out=xgs[slot], in_=s),
              ([] if l == 0 else [f"xs{g}"]), xres(slot))
        return slot

    def xres(slot):
        return [f"xg{slot}_{k}" for k in range(8)]

    def prep2(l, g, fix=True):
        slot = load_x(l, g)
        rmsnorm(l, 0, slot)
        cols = slice(g * TG, (g + 1) * TG)
        s1 = y1_d.rearrange("(c p) t -> p c t", p=128)[:, :, cols]
        s2 = y2_d.rearrange("(c p) t -> p c t", p=128)[:, :, cols]
        S.dma("sp", "ld_y1", lambda e, s_=s1: e.dma_start(out=ya_pre, in_=s_), [f"y1d{g}"], yares)
        S.dma("sp", "ld_y2", lambda e, s_=s2: e.dma_start(out=y2g, in_=s_), [f"y2d{g}"], y2res)
        if fix:
            fixup()
        return slot

    def fixup():
        for c in range(10):
            STT(ya_pre[:, c, :], y2g[:, c, :], hin[:, c:c + 1], ya_pre[:, c, :], ALU.mult, ALU.add,
                [f"y2{c}", "hin", f"ya{c}"], [f"ya{c}"])
    yares = [f"ya{c}" for c in range(10)]
    y2res = [f"y2{c}" for c in range(10)]

    for l in range(NL):
        S.dma("pool", "ld_gw", lambda e, l=l: e.dma_start(
            out=gwbf.rearrange("p a c m -> p (a c m)"), in_=gw_d[l]), (), ["gwbf"])
        S.dma("pool", "ld_ws", lambda e, l=l: e.dma_start(
            out=wsbf.rearrange("p g t -> p (g t)"), in_=ws_d[l]), (), ["wsbf"])
        S.dma("sp", "ld_bsb", lambda e, l=l: e.dma_start(out=bsb_v, in_=bsb_d[l]), (), bsb_r)
        MSET(wsbf[64:128, :, 0:64], 0.0, ["wsbf"])
        for g in range(8):
            ps, pr = nextps()
            MM(ps[:, 0:128], onesbf, wsbf[:, g, :], True, True, ["onesbf", "wsbf"], [pr])
            STT(Cg[:, g, :], ps[:, 0:128], pvc(l, 96 + g), bsb[:, g, :], ALU.mult, ALU.add,
                [pr, "pv"] + bsb_r, ["Cg"])
        MSET(hstate, 0.0, ["hstate"])
        MSET(pstate, 1.0, ["pstate"])
        win = win_d[l]

        S.op("act", lambda e: e.activation(out=sq3, in_=xh3, func=AF.Square), ["xh3"], ["sq3"])
        ps, pr = nextps()
        for k in range(8):
            MM(ps[:, 0:3], ones32, sq3[:, k, :], k == 0, k == 7, ["ones32", "sq3"], [pr])
        ACT(rs3, ps[:, 0:3], AF.Sqrt, [pr, "epsb"], ["rs3"], scale=1.0 / D, bias=epsb)
        S.op("dve", lambda e: e.reciprocal(out=rs3, in_=rs3), ["rs3"], ["rs3"])
        for k in range(8):
            STT(hh[:, k, :], xh3[:, k, :], pvc(l, k), rs3, ALU.mult, ALU.mult, ["xh3", "pv", "rs3"], ["hh"])

        for cp_ in range(5):
            W, wr = wget([(win[:, cp_ * 256: cp_ * 256 + 256], 256)], 8)
            for ci in range(2):
                c = 2 * cp_ + ci
                psh, phr = nextps()
                for k in range(8):
                    MM(psh[:, 0:3], W(k, ci * 128, 128), hh[:, k, :], k == 0, k == 7, [wr, "hh"], [phr])
                CP(halo[:, c, :], psh[:, 0:3], [phr], [f"halo{c}"])

        for g in range(NG):
            cols = slice(g * TG, (g + 1) * TG)
            slot = load_x(l, g)
            rmsnorm(l, 0, slot)
            st = {}

            def bufs(c):
                par = c % 2
                def tmp(idx, par=par):
                    return FR.f32((par * 8 + idx) * 2048, TG)
                return dict(r=tmp(0), i=tmp(1), a=tmp(2), a2=tmp(3), nrm=tmp(4), u=tmp(5), hs=tmp(6), gg=tmp(7),
                            xp=xr_pad[par], ac=acc[c % 4], xb=xr_bf[par])

            def S1(c):
                cp_, ci = divmod(c, 2)
                if ci == 0:
                    st[("W", cp_)] = wget([(win[:, cp_ * 256: cp_ * 256 + 256], 256),
                                           (win[:, DR + cp_ * 256: DR + cp_ * 256 + 256], 256)], 8)
                W, wr = st[("W", cp_)]
                ps1, p1r = pss[c % 2], f"ps{c % 2}"
                for k in range(8):
                    MM(ps1, W(k, ci * 128, 128), h[:, k, :], k == 0, k == 7, [wr, f"h{k}"], [p1r])
                st[("ps1", c)] = (ps1, p1r)

            def S2a(c):
                b = bufs(c)
                xp, xpr = b["xp"]; ac, acr = b["ac"]
                ps1, p1r = st[("ps1", c)]
                ACT(xp[:, 3:TG + 3], ps1, AF.Copy, [p1r], xpr)
                CP(xp[:, 0:3], halo[:, c, :], [f"halo{c}"], xpr)
                TS(ac, xp[:, 3:TG + 3], pvc(l, 8 + 30 + c), pvc(l, 48 + c), ALU.mult, ALU.add,
                   xpr + ["pv"], acr)
                for tap in (2, 1, 0):
                    STT(ac, xp[:, tap:tap + TG], pvc(l, 8 + tap * 10 + c), ac, ALU.mult, ALU.add,
                        xpr + ["pv"] + acr, acr)
                CP(halo[:, c, :], xp[:, TG:TG + 3], xpr, [f"halo{c}"])

            def S2b(c):
                b = bufs(c)
                ac, acr = b["ac"]; xb, xbr = b["xb"]
                ACT(xb, ac, AF.Copy, acr, xbr)

            def S3(c):
                cp_, ci = divmod(c, 2)
                W, wr = st[("W", cp_)]
                b = bufs(c)
                xb, xbr = b["xb"]
                psa, par_ = pss[2 + c % 2], f"ps{2 + c % 2}"
                MM(psa, gwbf[:, 0, c, :], xb, True, True, ["gwbf"] + xbr, [par_])
                psx, pxr = pss[4 + c % 2], f"ps{4 + c % 2}"
                MM(psx, gwbf[:, 1, c, :], xb, True, True, ["gwbf"] + xbr, [pxr])
                psg, pgr = pss[6 + c % 2], f"ps{6 + c % 2}"
                for k in range(8):
                    MM(psg, W(k, 256 + ci * 128, 128), h[:, k, :], k == 0, k == 7, [wr, f"h{k}"], [pgr])
                st[("g", c)] = (psa, par_, psx, pxr, psg, pgr)

            def S4a_te(c):
                b = bufs(c)
                psa, par_, psx, pxr, psg, pgr = st[("g", c)]
                r_, r_r = b["r"]; i_, i_r = b["i"]; a_, a_r = b["a"]; a2_, a2_r = b["a2"]
                ACT(r_, psa, AF.Tanh, [par_, "hba"], r_r, scale=0.5, bias=hba[:, l, c:c + 1])
                ACT(i_, psx, AF.Tanh, [pxr, "hbx"], i_r, scale=0.5, bias=hbx[:, l, c:c + 1])
                ACT(a_, r_, AF.Exp, r_r + ["hsc"], a_r, scale=hsc[:, l, c:c + 1], bias=hsc[:, l, c:c + 1])
                ACT(a2_, r_, AF.Exp, r_r + ["sc"], a2_r, scale=sc[:, l, c:c + 1], bias=sc[:, l, c:c + 1])

            def S4a_s(c):
                b = bufs(c)
                a2_, a2_r = b["a2"]; nrm, nrm_r = b["nrm"]
                ACT(nrm, a2_, AF.Sqrt, a2_r, nrm_r, scale=-0.25, bias=0.25)

            def S4a_g(c):
                b = bufs(c)
                psa, par_, psx, pxr, psg, pgr = st[("g", c)]
                gg, gg_r = b["gg"]
                ACT(gg, psg, AF.Gelu_apprx_tanh, [pgr], gg_r)

            def S4a_pair(c0):
                for fn in (S4a_te, S4a_s, S4a_g):
                    fn(c0)
                    fn(c0 + 1)

            def S4b(c):
                b = bufs(c)
                i_, i_r = b["i"]; a_, a_r = b["a"]; a2_, a2_r = b["a2"]
                nrm, nrm_r = b["nrm"]; u_, u_r = b["u"]; hs, hs_r = b["hs"]; gg, gg_r = b["gg"]
                ac, acr = b["ac"]
                STT(u_, i_, 1.0, nrm, ALU.add, ALU.mult, nrm_r + i_r, u_r)
                TT(u_, u_, ac, ALU.mult, u_r + acr, u_r)
                S.op("dve", lambda e: e.tensor_tensor_scan(
                    out=hs, data0=a_, data1=u_, initial=hstate[:, c:c + 1], op0=ALU.mult, op1=ALU.add),
                    a_r + u_r + ["hstate"], hs_r)
                CP(hstate[:, c:c + 1], hs[:, TG - 1:TG], hs_r, ["hstate"])
                pc, pc_r = a2_, a2_r
                S.op("dve", lambda e: e.tensor_tensor_scan(
                    out=pc, data0=a_, data1=zer, initial=pstate[:, c:c + 1], op0=ALU.mult, op1=ALU.add),
                    a_r + ["zer", "pstate"], pc_r)
                CP(pstate[:, c:c + 1], pc[:, TG - 1:TG], pc_r, ["pstate"])
                TT(ya_pre[:, c, :], hs, gg, ALU.mult, hs_r + gg_r, [f"ya{c}"])
                TT(y2g[:, c, :], pc, gg, ALU.mult, pc_r + gg_r, [f"y2{c}"])

            S1(0)
            S1(1)
            for p_ in range(5):
                c0, c1 = 2 * p_, 2 * p_ + 1
                S2a(c0)
                S2a(c1)
                if c0 + 2 < 10:
                    S1(c0 + 2)
                    S1(c1 + 2)
                if p_ >= 1:
                    S4a_pair(c0 - 2)
                S2b(c0)
                S2b(c1)
                S3(c0)
                S3(c1)
                if p_ >= 1:
                    S4b(c0 - 2)
                    S4b(c1 - 2)
            S4a_pair(8)
            S4b(8)
            S4b(9)
            d1 = y1_d.rearrange("(c p) t -> p c t", p=128)[:, :, cols]
            d2 = y2_d.rearrange("(c p) t -> p c t", p=128)[:, :, cols]
            S.dma("sp", "st_y1", lambda e, d=d1: e.dma_start(out=d, in_=ya_pre), yares, [f"y1d{g}"])
            S.dma("sp", "st_y2", lambda e, d=d2: e.dma_start(out=d, in_=y2g), y2res, [f"y2d{g}"])

        recv_h = exchange(hstate, ["hstate"], 10, hin, ["hin"])
        deferred = [lambda: (recv_h(), fixup())]

        order = [NG - 1] + list(range(NG - 1))
        nslot = prep2(l, order[0], fix=False)
        for oi, g in enumerate(order):
            cols = slice(g * TG, (g + 1) * TG)
            slot = nslot
            xg = xgs[slot]

            for vb in range(2):
                W, wr = wget([(win[:, 3584 + vb * 512: 3584 + (vb + 1) * 512], 512)], 8)
                for tt in range(4):
                    gvt, gvr = A2.f32(tt * 4096 + vb * 2048, 512)
                    ps, pr = nextps()
                    for k in range(8):
                        MM(ps, h[:, k, tt * 128:(tt + 1) * 128], W(k, 0, 512), k == 0, k == 7,
                           [wr, f"h{k}"], [pr])
                    ACT(gvt, ps, AF.Gelu_apprx_tanh, [pr], gvr)
            for ub in range(2):
                W, wr = wget([(win[:, 2560 + ub * 512: 2560 + (ub + 1) * 512], 512)], 8)
                for ji in range(4):
                    j = 4 * ub + ji
                    guj, gur = A1.f32(j * 2048, TG)
                    ps, pr = nextps()
                    for k in range(8):
                        MM(ps, W(k, ji * 128, 128), h[:, k, :], k == 0, k == 7, [wr, f"h{k}"], [pr])
                    ACT(guj, ps, AF.Gelu_apprx_tanh, [pr], gur)
            for tt in range(4):
                gvt, gvr = A2.f32(tt * 4096, 1024)
                vnt, vnr = A3.bf16(tt * 2048, 1024)
                S.op("dve", lambda e, tt=tt, gvt=gvt: e.bn_stats(out=bnst[:, tt, 0:6], in_=gvt[:, 0:512]),
                     gvr, [f"bnst{tt}"])
                S.op("dve", lambda e, tt=tt, gvt=gvt: e.bn_stats(out=bnst[:, tt, 6:12], in_=gvt[:, 512:1024]),
                     gvr, [f"bnst{tt}"])
                S.op("dve", lambda e, tt=tt: e.bn_aggr(out=mv[:, tt, :], in_=bnst[:, tt, :]),
                     [f"bnst{tt}"], [f"mv{tt}"])
                ACT(lsd[:, tt:tt + 1], mv[:, tt, 1:2], AF.Sqrt, [f"mv{tt}", "epsb"], [f"lsd{tt}"],
                    scale=1.0, bias=epsb)
                S.op("dve", lambda e, tt=tt: e.reciprocal(out=lrs[:, tt:tt + 1], in_=lsd[:, tt:tt + 1]),
                     [f"lsd{tt}"], [f"lrs{tt}"])
                TS(vnt, gvt, mv[:, tt, 0:1], lrs[:, tt:tt + 1], ALU.subtract, ALU.mult,
                   gvr + [f"mv{tt}", f"lrs{tt}"], vnr)
            for gi in range(8):
                ps, pr = nextps()
                for tt in range(4):
                    vnt, vnr = A3.bf16(tt * 2048, 1024)
                    MM(ps[:, tt * 128:(tt + 1) * 128], vnt[:, gi * 128:(gi + 1) * 128], wsbf[:, gi, :],
                       True, True, vnr + ["wsbf"], [pr], inc=(tt == 3))
                tm, tmr = tmpm[gi % 2]
                for tt in range(4):
                    STT(tm[:, tt * 128:(tt + 1) * 128], ps[:, tt * 128:(tt + 1) * 128], pvc(l, 88 + gi),
                        Cg[:, gi, :], ALU.mult, ALU.add, [pr, "pv", "Cg"], tmr)
                guj, gur = A1.f32(gi * 2048, TG)
                TT(yb_pre[:, gi, :], tm, guj, ALU.mult, tmr + gur, [f"yb{gi}"])

            while deferred:
                deferred.pop(0)()
            for ab in range(2):
                W, wr = wget([(win[:, 4608 + ab * 512: 4608 + (ab + 1) * 512], 512)], 8)
                for ji in range(4):
                    j = 4 * ab + ji
                    sgj, sgr = A2.f32(j * 2048, TG)
                    ps, pr = nextps()
                    for k in range(8):
                        MM(ps, W(k, ji * 128, 128), h[:, k, :], k == 0, k == 7, [wr, f"h{k}"], [pr])
                    ACT(sgj, ps, AF.Sigmoid, [pr], sgr)
            for ab in range(2):
                W, wr = wget([(wa_d[l][:, ab * 512:(ab + 1) * 512], 512)], 10)
                for ji in range(4):
                    j = 4 * ab + ji
                    sgj, sgr = A2.f32(j * 2048, TG)
                    ps, pr = nextps()
                    for c in range(10):
                        MM(ps, W(c, ji * 128, 128), ya_pre[:, c, :], c == 0, c == 9, [wr, f"ya{c}"], [pr])
                    TT(sgj, ps, sgj, ALU.mult, [pr] + sgr, sgr)
            for ab in range(2):
                W, wr = wget([(win[:, 5632 + ab * 512: 5632 + (ab + 1) * 512], 512)], 8)
                for ji in range(4):
                    j = 4 * ab + ji
                    s2j, s2r = A1.f32(j * 2048, TG)
                    ps, pr = nextps()
                    for k in range(8):
                        MM(ps, W(k, ji * 128, 128), h[:, k, :], k == 0, k == 7, [wr, f"h{k}"], [pr])
                    ACT(s2j, ps, AF.Sigmoid, [pr], s2r)
            for ab in range(2):
                W, wr = wget([(wb_d[l][:, ab * 512:(ab + 1) * 512], 512)], 8)
                for ji in range(4):
                    j = 4 * ab + ji
                    sgj, sgr = A2.f32(j * 2048, TG)
                    s2j, s2r = A1.f32(j * 2048, TG)
                    mj, mr = A3.bf16(j * 1024, TG)
                    ps, pr = nextps()
                    for c in range(8):
                        MM(ps, W(c, ji * 128, 128), yb_pre[:, c, :], c == 0, c == 7, [wr, f"yb{c}"], [pr])
                    TT(s2j, ps, s2j, ALU.mult, [pr] + s2r, s2r)
                    TT(mj, sgj, s2j, ALU.add, sgr + s2r, mr)
            for ob in range(2):
                W, wr = wget([(wo_d[l][:, ob * 512:(ob + 1) * 512], 512)], 8)
                for ji in range(4):
                    j = 4 * ob + ji
                    ps, pr = nextps()
                    for k in range(8):
                        mk, mkr = A3.bf16(k * 1024, TG)
                        MM(ps, W(k, ji * 128, 128), mk, k == 0, k == 7, [wr] + mkr, [pr])
                    TT(xg[:, j, :], xg[:, j, :], ps, ALU.add, [pr, f"xg{slot}_{j}"], [f"xg{slot}_{j}"])

            rmsnorm(l, 104, slot)
            for ub in range(8):
                W, wr = wget([(wu_d[l][:, ub * 512:(ub + 1) * 512], 512)], 8)
                for ji in range(4):
                    jf = 4 * ub + ji
                    fj, fr = FR.bf16(jf * 1024, TG)
                    ps, pr = nextps()
                    for k in range(8):
                        MM(ps, W(k, ji * 128, 128), h[:, k, :], k == 0, k == 7, [wr, f"h{k}"], [pr])
                    rr, rrr = rl[jf % 2]
                    ACT(rr, ps, AF.Relu, [pr], rrr)
                    TT(fj, rr, rr, ALU.mult, rrr, fr)
            if oi + 1 < len(order):
                nslot = prep2(l, order[oi + 1])
            for j in range(8):
                W, wr = wget([(wd_d[l][:, j * 128:(j + 1) * 128], 128)], 32)
                ps, pr = nextps()
                for kf in range(32):
                    fk, fkr = FR.bf16(kf * 1024, TG)
                    MM(ps, W(kf, 0, 128), fk, kf == 0, kf == 31, [wr] + fkr, [pr])
                TT(xg[:, j, :], xg[:, j, :], ps, ALU.add, [pr, f"xg{slot}_{j}"], [f"xg{slot}_{j}"])

            if l < NL - 1:
                dst = xs_d.rearrange("(k p) t -> p k t", p=128)[:, :, cols]
                S.dma("sp", f"st_x{slot}", lambda e, d=dst, xg=xg: e.dma_start(out=d, in_=xg), xres(slot), [f"xs{g}"])
                if g == NG - 1:
                    CP(sq3, xg[:, :, TG - 3:TG], xres(slot), ["sq3"])
                    deferred.append(exchange(sq3.rearrange("p k t -> p (k t)"), ["sq3"], 24,
                                             xh3.rearrange("p k t -> p (k t)"), ["xh3"]))
            else:
                rmsnorm(l, None, slot)
                yres = []
                for k in range(8):
                    yk, ykr = A1.f32(k * 2048, TG)
                    STT(yk, xg[:, k, :], pvc(l, 112 + k), rstd, ALU.mult, ALU.mult,
                        [f"xg{slot}_{k}", "pv", "rstd"], ykr)
                    yres += ykr
                yv = A1.t32.rearrange("p (k t) -> p k t", k=8)
                dst = yT_d.rearrange("(k p) t -> p k t", p=128)[:, :, cols]
                S.dma("sp", "st_y", lambda e, d=dst, yv=yv: e.dma_start(out=d, in_=yv), yres, [f"y{g}"])

        while deferred:
            deferred.pop(0)()

    if dry:
        return recorded
    S.finish(["st_y", "st_x0", "st_x1", "st_y1", "st_y2", "ex"])
    S.emit()
    return nc


def _pack_params(p):
    f = np.float32
    pv = np.zeros((NL, 128, NPV), f)
    gw = np.zeros((NL, 128, 2, 10, 128), f)
    for l in range(NL):
        def put(col0, vec):
            n = vec.shape[0] // 128
            pv[l, :, col0:col0 + n] = vec.reshape(n, 128).T
        put(0, p["norm_mix_g"][l])
        for tap in range(4):
            put(8 + tap * 10, p["conv_w"][l, tap])
        put(48, p["conv_b"][l])
        put(58, p["lru_b_a"][l].reshape(-1))
        put(68, p["lru_b_x"][l].reshape(-1))
        put(78, p["lru_lambda"][l])
        put(88, p["sgu_ln_g"][l])
        put(96, p["sgu_ln_b"][l])
        put(104, p["norm_ffn_g"][l])
        put(112, p["final_norm_g"])
        for hd in range(20):
            c, hh = divmod(hd, 2)
            gw[l, hh * 64:(hh + 1) * 64, 0, c, hh * 64:(hh + 1) * 64] = p["lru_w_a"][l, hd]
            gw[l, hh * 64:(hh + 1) * 64, 1, c, hh * 64:(hh + 1) * 64] = p["lru_w_x"][l, hd]
    wsT = np.ascontiguousarray(np.transpose(p["sgu_w_s"], (0, 3, 1, 2))).reshape(NL, 128, 8 * 128)
    bsb = np.ascontiguousarray(np.broadcast_to(p["sgu_b_s"].reshape(NL, 1, 8 * 128), (NL, 128, 8 * 128)))
    return pv, gw.reshape(NL, 128, 2 * 10 * 128), wsT.astype(f), bsb.astype(f)


_CACHE = {}


def _program(T):
    if T not in _CACHE:
        plan = build(T, None)
        _CACHE[T] = build(T, plan)
    return _CACHE[T]


def kernel(**inputs):
    p = {k: np.asarray(v, dtype=np.float32) for k, v in inputs.items()}
    x = p["x"]
    B, SEQ, _ = x.shape
    pv, gw, wsT, bsb = _pack_params(p)
    n = 8
    halves = n // B
    T = SEQ // halves
    nc = _program(T)
    shared = {
        "w_in": p["w_in"], "w_branch_a": p["w_branch_a"], "w_branch_b": p["w_branch_b"],
        "w_out": p["w_out"], "w_up": p["w_up"], "w_down": p["w_down"],
        "pv": pv, "gw": gw, "wsT": wsT, "bsb": bsb,
    }
    in_maps = []
    for c in range(n):
        b, hf = divmod(c, halves)
        m = dict(shared)
        m["xT"] = np.ascontiguousarray(x[b, hf * T:(hf + 1) * T].T)
        xh = np.zeros((D, 3), np.float32)
        oh = np.zeros((2, 1), np.float32)
        sel = np.zeros((2, 1), np.float32)
        oh[hf, 0] = 1.0
        if hf > 0:
            xh[:] = x[b, hf * T - 3:hf * T].T
            sel[hf - 1, 0] = 1.0
        m["xh"], m["oh"], m["sel"] = xh, oh, sel
        in_maps.append(m)
    res = run_bass_kernel_spmd(nc, in_maps, core_ids=list(range(n)))
    out = np.empty((B, SEQ, D), np.float32)
    for c in range(n):
        b, hf = divmod(c, halves)
        out[b, hf * T:(hf + 1) * T] = res.results[c]["yT"].T
    return out
```

```python
import numpy as np
import concourse.bass as bass
import concourse.mybir as mybir
from concourse.bass_utils import run_bass_kernel_spmd

F32 = mybir.dt.float32
BF16 = mybir.dt.bfloat16
AF = mybir.ActivationFunctionType
ALU = mybir.AluOpType

D = 1024
DR = 1280
DS = 1024
DFF = 4096
DIN = 6656
NL = 2
EPS = 1e-6
TG = 512
NPV = 120
ENGS = ("pe", "act", "dve", "pool", "sp")


class Sched:
    def __init__(self, nc):
        self.nc = nc
        self.q = {e: [] for e in ENGS}
        self.sems = {}
        self.cnt = {}
        self.seen = {e: {} for e in ENGS}
        self.last_w = {}
        self.readers = {}

    def sem(self, key):
        if key not in self.sems:
            self.sems[key] = self.nc.alloc_semaphore(name="s_" + key)
            self.cnt[key] = 0
        return self.sems[key]

    def _deps(self, engine, reads, writes):
        best = {}
        def add(ev):
            sk, v = ev
            if engine == "pe" and sk == "pe":
                return
            if best.get(sk, 0) < v:
                best[sk] = v
        for r in reads:
            ev = self.last_w.get(r)
            if ev is not None:
                add(ev)
        for w in writes:
            ev = self.last_w.get(w)
            if ev is not None:
                add(ev)
            for ev in self.readers.get(w, ()):
                add(ev)
        waits = []
        for sk, v in best.items():
            if self.seen[engine].get(sk, 0) < v:
                self.seen[engine][sk] = v
                waits.append((sk, v))
        return waits

    def _record(self, ev, reads, writes):
        for r in reads:
            self.readers.setdefault(r, []).append(ev)
        for w in writes:
            self.last_w[w] = ev
            self.readers[w] = []

    def op(self, engine, fn, reads=(), writes=(), inc=True):
        self.sem(engine)
        waits = self._deps(engine, reads, writes)
        ev = (engine, self.cnt[engine] + 1)
        if inc:
            self.cnt[engine] += 1
        else:
            assert engine == "pe"
        self._record(ev, reads, writes)
        self.q[engine].append((waits, fn, (engine, 1) if inc else None))

    def dma(self, queue, semkey, fn, reads=(), writes=()):
        self.sem(semkey)
        waits = self._deps(queue, reads, writes)
        self.cnt[semkey] += 16
        ev = (semkey, self.cnt[semkey])
        self._record(ev, reads, writes)
        self.q[queue].append((waits, fn, (semkey, 16)))

    def finish(self, dma_semkeys):
        waits = [(sk, self.cnt[sk]) for sk in dma_semkeys if sk in self.cnt]
        self.q["sp"].append((waits, None, None))

    def emit(self):
        nc = self.nc
        sems = self.sems

        def run(eng, items):
            for waits, fn, inc in items:
                for sk, v in waits:
                    eng.wait_ge(sems[sk], v)
                if fn is None:
                    continue
                ins = fn(eng)
                if inc is not None:
                    ins.then_inc(sems[inc[0]], inc[1])

        with nc.Block() as block:
            @block.sync
            def _(e):
                run(e, self.q["sp"])

            @block.tensor
            def _(e):
                run(e, self.q["pe"])

            @block.scalar
            def _(e):
                run(e, self.q["act"])

            @block.vector
            def _(e):
                run(e, self.q["dve"])

            @block.gpsimd
            def _(e):
                run(e, self.q["pool"])


class Region:
    def __init__(self, nc, name, nbytes):
        self.name = name
        self.t32 = nc.alloc_sbuf_tensor("rg_" + name, [128, nbytes // 4], F32).ap()
        self.t16 = self.t32.bitcast(BF16)

    def _res(self, off, nb):
        return [f"{self.name}.{u}" for u in range(off // 1024, (off + nb + 1023) // 1024)]

    def f32(self, off, n):
        return self.t32[:, off // 4: off // 4 + n], self._res(off, n * 4)

    def bf16(self, off, n):
        return self.t16[:, off // 2: off // 2 + n], self._res(off, n * 2)


def build(T, plan=None):
    dry = plan is None
    NG = T // TG
    nc = bass.Bass("TRN2", target_bir_lowering=False)
    S = Sched(nc)

    def din(name, shape):
        return nc.dram_tensor(name, shape, F32, kind="ExternalInput").ap()

    xT_d = din("xT", [D, T])
    win_d = din("w_in", [NL, D, DIN])
    wa_d = din("w_branch_a", [NL, DR, D])
    wb_d = din("w_branch_b", [NL, DS, D])
    wo_d = din("w_out", [NL, D, D])
    wu_d = din("w_up", [NL, D, DFF])
    wd_d = din("w_down", [NL, DFF, D])
    pv_d = din("pv", [NL, 128, NPV])
    gw_d = din("gw", [NL, 128, 2 * 10 * 128])
    ws_d = din("wsT", [NL, 128, 8 * 128])
    bsb_d = din("bsb", [NL, 128, 8 * 128])
    xh_d = din("xh", [D, 3])
    oh_d = din("oh", [2, 1])
    sel_d = din("sel", [2, 1])
    yT_d = nc.dram_tensor("yT", [D, T], F32, kind="ExternalOutput").ap()
    xs_d = nc.dram_tensor("xs", [D, T], F32, kind="Internal").ap()
    y1_d = nc.dram_tensor("y1s", [DR, T], BF16, kind="Internal").ap()
    y2_d = nc.dram_tensor("y2s", [DR, T], BF16, kind="Internal").ap()

    def sb(name, shape, dt=F32):
        return nc.alloc_sbuf_tensor("sb_" + name, shape, dt).ap()

    ones32 = sb("ones32", [128, 128])
    onesbf = sb("onesbf", [128, 128], BF16)
    epsb = sb("epsb", [128, 1])
    pv = sb("pv", [128, NL, NPV])
    sct = sb("sct", [128, NL, 10])
    sc = sb("sc", [128, NL, 10])
    sc2 = sb("sc2", [128, NL, 10])
    hsc = sb("hsc", [128, NL, 10])
    hba = sb("hba", [128, NL, 10])
    hbx = sb("hbx", [128, NL, 10])
    gwbf = sb("gwbf", [128, 2, 10, 128], BF16)
    wsbf = sb("wsbf", [128, 8, 128], BF16)
    Cg = sb("Cg", [128, 8, 128])
    halo = sb("halo", [128, 10, 3])
    hstate = sb("hstate", [128, 10])
    pstate = sb("pstate", [128, 10])
    hin = sb("hin", [128, 10])
    xh3 = sb("xh3", [128, 8, 3])
    sq3 = sb("sq3", [128, 8, 3])
    rs3 = sb("rs3", [128, 3])
    hh = sb("hh", [128, 8, 3], BF16)
    zer = sb("zer", [128, TG])
    oh = sb("oh", [2, 1])
    sel = sb("sel", [2, 1])
    y2g = sb("y2g", [128, 10, TG], BF16)
    xgs = [sb(f"xg{i}", [128, 8, TG]) for i in range(2)]
    cur = {"slot": 0, "n": 0}
    h = sb("h", [128, 8, TG], BF16)
    h1 = sb("h1", [128, 8, TG], BF16)
    ya_pre = sb("ya_pre", [128, 10, TG], BF16)
    yb_pre = sb("yb_pre", [128, 8, TG], BF16)
    rstd = sb("rstd", [128, TG])
    bnst = sb("bnst", [128, 4, 12])
    mv = sb("mv", [128, 4, 2])
    lsd = sb("lsd", [128, 4])
    lrs = sb("lrs", [128, 4])
    A1 = Region(nc, "A1", 16384)
    A2 = Region(nc, "A2", 16384)
    A3 = Region(nc, "A3", 8192)
    FR = Region(nc, "FR", 32768)
    bsb_v, bsb_r = A2.f32(0, 1024)
    bsb = bsb_v.rearrange("p (g t) -> p g t", g=8)
    acc = [A2.f32(i * 2048, TG) for i in range(2)] + [A2.f32(12288 + i * 2048, TG) for i in range(2)]
    xr_pad = [A2.f32(4096 + i * 3072, TG + 3) for i in range(2)]
    xr_bf = [A2.bf16(10240 + i * 1024, TG) for i in range(2)]
    tmpm = [A2.f32(i * 2048, TG) for i in range(2)]
    rl = [A2.f32(4096 + i * 2048, TG) for i in range(2)]
    WCAP = 5120
    NWB = 4
    wst = [sb(f"wst{i}", [128, WCAP], BF16) for i in range(NWB)]
    pss = [nc.alloc_psum_tensor(f"ps{i}", [128, TG], F32).ap() for i in range(8)]
    psn = [0]

    def nextps():
        i = psn[0] % 8
        psn[0] += 1
        return pss[i], f"ps{i}"

    def MM(out, lhsT, rhs, start, stop, reads, writes, inc=None):
        if inc is None:
            inc = stop
        S.op("pe", lambda e: e.matmul(out, lhsT=lhsT, rhs=rhs, start=start, stop=stop),
             reads, writes, inc=inc)

    def ACT(out, in_, func, reads, writes, **kw):
        S.op("act", lambda e: e.activation(out=out, in_=in_, func=func, **kw), reads, writes)

    def TT(out, in0, in1, op, reads, writes):
        S.op("dve", lambda e: e.tensor_tensor(out=out, in0=in0, in1=in1, op=op), reads, writes)

    def TS(out, in0, s1, s2, op0, op1, reads, writes):
        S.op("dve", lambda e: e.tensor_scalar(out=out, in0=in0, scalar1=s1, scalar2=s2, op0=op0, op1=op1),
             reads, writes)

    def STT(out, in0, scalar, in1, op0, op1, reads, writes):
        S.op("dve", lambda e: e.scalar_tensor_tensor(out=out, in0=in0, scalar=scalar, in1=in1, op0=op0, op1=op1),
             reads, writes)

    def CP(out, in_, reads, writes):
        S.op("dve", lambda e: e.tensor_copy(out=out, in_=in_), reads, writes)

    def MSET(ap, val, writes):
        S.op("dve", lambda e: e.memset(ap, val), (), writes)

    recorded = []
    wstate = {"n": 0, "issued": 0}

    def issue(n):
        pieces, kt, ncols = plan[n]
        bi = n % NWB
        view = wst[bi][:, 0:kt * ncols].rearrange("p (k c) -> p k c", k=kt)
        c0 = 0
        for (src2d, nc_) in pieces:
            dst = view[:, :, c0:c0 + nc_]
            src = src2d.rearrange("(k p) c -> p k c", p=128)
            S.dma("pool", f"w{bi}", lambda e, d=dst, s=src: e.dma_start(out=d, in_=s), (), [f"wst{bi}"])
            c0 += nc_

    def wget(pieces, kt):
        ncols = sum(p[1] for p in pieces)
        assert kt * ncols <= WCAP
        n = wstate["n"]
        wstate["n"] += 1
        if dry:
            recorded.append((pieces, kt, ncols))
            bi = n % NWB
        else:
            while wstate["issued"] < min(n + 3, len(plan)):
                issue(wstate["issued"])
                wstate["issued"] += 1
            bi = n % NWB
        view = wst[bi][:, 0:kt * ncols].rearrange("p (k c) -> p k c", k=kt)

        def W(k, c0, n_):
            return view[:, k, c0:c0 + n_]
        return W, f"wst{bi}"

    def pvc(l, col):
        return pv[:, l, col:col + 1]

    MSET(ones32, 1.0, ["ones32"])
    MSET(onesbf, 1.0, ["onesbf"])
    MSET(epsb, EPS, ["epsb"])
    MSET(zer, 0.0, ["zer"])
    S.dma("sp", "ld_misc", lambda e: e.dma_start(out=oh, in_=oh_d), (), ["oh"])
    S.dma("sp", "ld_sel", lambda e: e.dma_start(out=sel, in_=sel_d), (), ["sel"])
    S.dma("sp", "ld_xh", lambda e: e.dma_start(out=xh3, in_=xh_d.rearrange("(k p) t -> p k t", p=128)),
          (), ["xh3"])
    nex = [0]

    def exchange(src, srcres, f, dst, dstres):
        i = nex[0]
        nex[0] += 1
        n = 128 * f
        scr = nc.dram_tensor(f"exs{i}", [1, n], F32, kind="Internal").ap()
        cin = nc.dram_tensor(f"exi{i}", [2, n], F32, kind="Internal").ap()
        cout = nc.dram_tensor(f"exo{i}", [2, n], F32, kind="Internal").ap()
        G, gr = FR.t32[0:2, 0:n], FR._res(0, n * 4)
        S.dma("sp", "ex", lambda e: e.dma_start(out=scr.rearrange("o (p f) -> (o p) f", p=128), in_=src),
              srcres, [f"exs{i}"])
        S.dma("sp", "ex", lambda e: e.dma_start(out=G, in_=scr.broadcast_to([2, n])), [f"exs{i}"], gr)
        S.op("dve", lambda e: e.tensor_scalar_mul(out=G, in0=G, scalar1=oh[0:2, 0:1]), gr + ["oh"], gr)
        S.dma("sp", "ex", lambda e: e.dma_start(out=cin, in_=G), gr, [f"exi{i}"])
        S.sem("cc")
        waits = S._deps("pool", [f"exi{i}"], [f"exo{i}"])
        for wk in [k_ for k_ in S.cnt if k_ not in ENGS and k_ != "cc"]:
            if S.seen["pool"].get(wk, 0) < S.cnt[wk]:
                S.seen["pool"][wk] = S.cnt[wk]
                waits.append((wk, S.cnt[wk]))
        S.cnt["cc"] += 1
        S._record(("cc", S.cnt["cc"]), [f"exi{i}"], [f"exo{i}"])
        S.q["pool"].append((waits, lambda e: e.collective_compute(
            "AllReduce", ALU.add, replica_groups=[[0, 1], [2, 3], [4, 5], [6, 7]],
            ins=[cin.opt()], outs=[cout.opt()]),
            ("cc", 1)))
        S.seen["pool"]["cc"] = S.cnt["cc"]
        S.q["pool"].append(([("cc", S.cnt["cc"])], None, None))
        def recv():
            S.dma("sp", "ex", lambda e: e.dma_start(out=G, in_=cout), [f"exo{i}"], gr)
            Gv = G.rearrange("r (p f) -> r p f", p=128)
            ps, pr = nextps()
            for j in range(f):
                MM(ps[:, j:j + 1], Gv[:, :, j], sel[0:2, 0:1], True, True, gr + ["sel"], [pr], inc=(j == f - 1))
            CP(dst, ps[:, 0:f], [pr], dstres)
        return recv
    S.dma("sp", "ld_pv", lambda e: e.dma_start(out=pv, in_=pv_d.rearrange("l p n -> p l n")), (), ["pv"])
    ACT(sct, pv[:, :, 78:88], AF.Exp, ["pv"], ["sct"], scale=-1.0)
    ACT(sct, sct, AF.Ln, ["sct"], ["sct"], scale=1.0, bias=1.0)
    S.op("dve", lambda e: e.tensor_scalar_mul(out=sc, in0=sct, scalar1=-8.0), ["sct"], ["sc"])
    S.op("dve", lambda e: e.tensor_scalar_mul(out=sc2, in0=sct, scalar1=-16.0), ["sct"], ["sc2"])
    S.op("dve", lambda e: e.tensor_scalar_mul(out=hsc, in0=sct, scalar1=-4.0), ["sct"], ["hsc"])
    S.op("dve", lambda e: e.tensor_scalar_mul(out=hba, in0=pv[:, :, 58:68], scalar1=0.5), ["pv"], ["hba"])
    S.op("dve", lambda e: e.tensor_scalar_mul(out=hbx, in0=pv[:, :, 68:78], scalar1=0.5), ["pv"], ["hbx"])

    def rmsnorm(l, gbase, slot, hb=None):
        xg = xgs[slot]
        for k in range(8):
            sqk, r = A1.f32(k * 2048, TG)
            ACT(sqk, xg[:, k, :], AF.Square, [f"xg{slot}_{k}"], r)
        ps, pr = nextps()
        for k in range(8):
            sqk, r = A1.f32(k * 2048, TG)
            MM(ps, ones32, sqk, k == 0, k == 7, ["ones32"] + r, [pr])
        ACT(rstd, ps, AF.Sqrt, [pr, "epsb"], ["rstd"], scale=1.0 / D, bias=epsb)
        S.op("dve", lambda e: e.reciprocal(out=rstd, in_=rstd), ["rstd"], ["rstd"])
        if gbase is not None:
            ht, hp = hb if hb is not None else (h, "h")
            for k in range(8):
                STT(ht[:, k, :], xg[:, k, :], pvc(l, gbase + k), rstd, ALU.mult, ALU.mult,
                    [f"xg{slot}_{k}", "pv", "rstd"], [f"{hp}{k}"])

    hres = [f"h{k}" for k in range(8)]

    def load_x(l, g):
        cur["n"] += 1
        slot = cur["n"] % 2
        cols = slice(g * TG, (g + 1) * TG)
        src = (xT_d if l == 0 else xs_d).rearrange("(k p) t -> p k t", p=128)[:, :, cols]
        S.dma("sp", f"ld_x{slot}", lambda e, s=src, slot=slot: e.dma_start(out=xgs[slot], in_=s),
              ([] if l == 0 else [f"xs{g}"]), xres(slot))
        return slot

    def xres(slot):
        return [f"xg{slot}_{k}" for k in range(8)]

    def prep2(l, g, fix=True):
        slot = load_x(l, g)
        rmsnorm(l, 0, slot)
        cols = slice(g * TG, (g + 1) * TG)
        s1 = y1_d.rearrange("(c p) t -> p c t", p=128)[:, :, cols]
        s2 = y2_d.rearrange("(c p) t -> p c t", p=128)[:, :, cols]
        S.dma("sp", "ld_y1", lambda e, s_=s1: e.dma_start(out=ya_pre, in_=s_), [f"y1d{g}"], yares)
        S.dma("sp", "ld_y2", lambda e, s_=s2: e.dma_start(out=y2g, in_=s_), [f"y2d{g}"], y2res)
        if fix:
            fixup()
        return slot

    def fixup():
        for c in range(10):
            STT(ya_pre[:, c, :], y2g[:, c, :], hin[:, c:c + 1], ya_pre[:, c, :], ALU.mult, ALU.add,
                [f"y2{c}", "hin", f"ya{c}"], [f"ya{c}"])
    yares = [f"ya{c}" for c in range(10)]
    y2res = [f"y2{c}" for c in range(10)]

    for l in range(NL):
        S.dma("pool", "ld_gw", lambda e, l=l: e.dma_start(
            out=gwbf.rearrange("p a c m -> p (a c m)"), in_=gw_d[l]), (), ["gwbf"])
        S.dma("pool", "ld_ws", lambda e, l=l: e.dma_start(
            out=wsbf.rearrange("p g t -> p (g t)"), in_=ws_d[l]), (), ["wsbf"])
        S.dma("sp", "ld_bsb", lambda e, l=l: e.dma_start(out=bsb_v, in_=bsb_d[l]), (), bsb_r)
        MSET(wsbf[64:128, :, 0:64], 0.0, ["wsbf"])
        for g in range(8):
            ps, pr = nextps()
            MM(ps[:, 0:128], onesbf, wsbf[:, g, :], True, True, ["onesbf", "wsbf"], [pr])
            STT(Cg[:, g, :], ps[:, 0:128], pvc(l, 96 + g), bsb[:, g, :], ALU.mult, ALU.add,
                [pr, "pv"] + bsb_r, ["Cg"])
        MSET(hstate, 0.0, ["hstate"])
        MSET(pstate, 1.0, ["pstate"])
        win = win_d[l]

        S.op("act", lambda e: e.activation(out=sq3, in_=xh3, func=AF.Square), ["xh3"], ["sq3"])
        ps, pr = nextps()
        for k in range(8):
            MM(ps[:, 0:3], ones32, sq3[:, k, :], k == 0, k == 7, ["ones32", "sq3"], [pr])
        ACT(rs3, ps[:, 0:3], AF.Sqrt, [pr, "epsb"], ["rs3"], scale=1.0 / D, bias=epsb)
        S.op("dve", lambda e: e.reciprocal(out=rs3, in_=rs3), ["rs3"], ["rs3"])
        for k in range(8):
            STT(hh[:, k, :], xh3[:, k, :], pvc(l, k), rs3, ALU.mult, ALU.mult, ["xh3", "pv", "rs3"], ["hh"])

        for cp_ in range(5):
            W, wr = wget([(win[:, cp_ * 256: cp_ * 256 + 256], 256)], 8)
            for ci in range(2):
                c = 2 * cp_ + ci
                psh, phr = nextps()
                for k in range(8):
                    MM(psh[:, 0:3], W(k, ci * 128, 128), hh[:, k, :], k == 0, k == 7, [wr, "hh"], [phr])
                CP(halo[:, c, :], psh[:, 0:3], [phr], [f"halo{c}"])

        for g in range(NG):
            cols = slice(g * TG, (g + 1) * TG)
            hbs = [(h, "h"), (h1, "hB")]
            hcur, hpre = hbs[g % 2]
            rmsnorm(l, 0, load_x(l, g), hbs[g % 2])
            st = {}

            def bufs(c):
                par = c % 2
                def tmp(idx, par=par):
                    return FR.f32((par * 8 + idx) * 2048, TG)
                return dict(r=tmp(0), i=tmp(1), a=tmp(2), a2=tmp(3), nrm=tmp(4), u=tmp(5), hs=tmp(6), gg=tmp(7),
                            xp=xr_pad[par], ac=acc[c % 4], xb=xr_bf[par])

            def S1(c):
                cp_, ci = divmod(c, 2)
                if ci == 0:
                    st[("W", cp_)] = wget([(win[:, cp_ * 256: cp_ * 256 + 256], 256),
                                           (win[:, DR + cp_ * 256: DR + cp_ * 256 + 256], 256)], 8)
                W, wr = st[("W", cp_)]
                ps1, p1r = pss[c % 2], f"ps{c % 2}"
                for k in range(8):
                    MM(ps1, W(k, ci * 128, 128), hcur[:, k, :], k == 0, k == 7, [wr, f"{hpre}{k}"], [p1r])
                st[("ps1", c)] = (ps1, p1r)

            def S2a(c):
                b = bufs(c)
                xp, xpr = b["xp"]; ac, acr = b["ac"]
                ps1, p1r = st[("ps1", c)]
                ACT(xp[:, 3:TG + 3], ps1, AF.Copy, [p1r], xpr)
                CP(xp[:, 0:3], halo[:, c, :], [f"halo{c}"], xpr)
                TS(ac, xp[:, 3:TG + 3], pvc(l, 8 + 30 + c), pvc(l, 48 + c), ALU.mult, ALU.add,
                   xpr + ["pv"], acr)
                for tap in (2, 1, 0):
                    STT(ac, xp[:, tap:tap + TG], pvc(l, 8 + tap * 10 + c), ac, ALU.mult, ALU.add,
                        xpr + ["pv"] + acr, acr)
                CP(halo[:, c, :], xp[:, TG:TG + 3], xpr, [f"halo{c}"])

            def S2b(c):
                b = bufs(c)
                ac, acr = b["ac"]; xb, xbr = b["xb"]
                ACT(xb, ac, AF.Copy, acr, xbr)

            def S3(c):
                cp_, ci = divmod(c, 2)
                W, wr = st[("W", cp_)]
                b = bufs(c)
                xb, xbr = b["xb"]
                psa, par_ = pss[2 + c % 2], f"ps{2 + c % 2}"
                MM(psa, gwbf[:, 0, c, :], xb, True, True, ["gwbf"] + xbr, [par_])
                psx, pxr = pss[4 + c % 2], f"ps{4 + c % 2}"
                MM(psx, gwbf[:, 1, c, :], xb, True, True, ["gwbf"] + xbr, [pxr])
                psg, pgr = pss[6 + c % 2], f"ps{6 + c % 2}"
                for k in range(8):
                    MM(psg, W(k, 256 + ci * 128, 128), hcur[:, k, :], k == 0, k == 7, [wr, f"{hpre}{k}"], [pgr])
                st[("g", c)] = (psa, par_, psx, pxr, psg, pgr)

            def S4a_te(c):
                b = bufs(c)
                psa, par_, psx, pxr, psg, pgr = st[("g", c)]
                r_, r_r = b["r"]; i_, i_r = b["i"]; a_, a_r = b["a"]; a2_, a2_r = b["a2"]
                ACT(r_, psa, AF.Tanh, [par_, "hba"], r_r, scale=0.5, bias=hba[:, l, c:c + 1])
                ACT(i_, psx, AF.Tanh, [pxr, "hbx"], i_r, scale=0.5, bias=hbx[:, l, c:c + 1])
                ACT(a_, r_, AF.Exp, r_r + ["hsc"], a_r, scale=hsc[:, l, c:c + 1], bias=hsc[:, l, c:c + 1])
                ACT(a2_, r_, AF.Exp, r_r + ["sc"], a2_r, scale=sc[:, l, c:c + 1], bias=sc[:, l, c:c + 1])

            def S4a_s(c):
                b = bufs(c)
                a2_, a2_r = b["a2"]; nrm, nrm_r = b["nrm"]
                ACT(nrm, a2_, AF.Sqrt, a2_r, nrm_r, scale=-0.25, bias=0.25)

            def S4a_g(c):
                b = bufs(c)
                psa, par_, psx, pxr, psg, pgr = st[("g", c)]
                gg, gg_r = b["gg"]
                ACT(gg, psg, AF.Gelu_apprx_tanh, [pgr], gg_r)

            def S4a_pair(c0):
                for fn in (S4a_te, S4a_s, S4a_g):
                    fn(c0)
                    fn(c0 + 1)

            def S4b(c):
                b = bufs(c)
                i_, i_r = b["i"]; a_, a_r = b["a"]; a2_, a2_r = b["a2"]
                nrm, nrm_r = b["nrm"]; u_, u_r = b["u"]; hs, hs_r = b["hs"]; gg, gg_r = b["gg"]
                ac, acr = b["ac"]
                STT(u_, i_, 1.0, nrm, ALU.add, ALU.mult, nrm_r + i_r, u_r)
                TT(u_, u_, ac, ALU.mult, u_r + acr, u_r)
                S.op("dve", lambda e: e.tensor_tensor_scan(
                    out=hs, data0=a_, data1=u_, initial=hstate[:, c:c + 1], op0=ALU.mult, op1=ALU.add),
                    a_r + u_r + ["hstate"], hs_r)
                CP(hstate[:, c:c + 1], hs[:, TG - 1:TG], hs_r, ["hstate"])
                pc, pc_r = a2_, a2_r
                S.op("dve", lambda e: e.tensor_tensor_scan(
                    out=pc, data0=a_, data1=zer, initial=pstate[:, c:c + 1], op0=ALU.mult, op1=ALU.add),
                    a_r + ["zer", "pstate"], pc_r)
                CP(pstate[:, c:c + 1], pc[:, TG - 1:TG], pc_r, ["pstate"])
                TT(ya_pre[:, c, :], hs, gg, ALU.mult, hs_r + gg_r, [f"ya{c}"])
                TT(y2g[:, c, :], pc, gg, ALU.mult, pc_r + gg_r, [f"y2{c}"])

            S1(0)
            S1(1)
            for p_ in range(5):
                c0, c1 = 2 * p_, 2 * p_ + 1
                S2a(c0)
                S2a(c1)
                if c0 + 2 < 10:
                    S1(c0 + 2)
                    S1(c1 + 2)
                if p_ >= 1:
                    S4a_pair(c0 - 2)
                S2b(c0)
                S2b(c1)
                S3(c0)
                S3(c1)
                if p_ >= 1:
                    S4b(c0 - 2)
                    S4b(c1 - 2)
            S4a_pair(8)
            S4b(8)
            S4b(9)
            d1 = y1_d.rearrange("(c p) t -> p c t", p=128)[:, :, cols]
            d2 = y2_d.rearrange("(c p) t -> p c t", p=128)[:, :, cols]
            S.dma("sp", "st_y1", lambda e, d=d1: e.dma_start(out=d, in_=ya_pre), yares, [f"y1d{g}"])
            S.dma("sp", "st_y2", lambda e, d=d2: e.dma_start(out=d, in_=y2g), y2res, [f"y2d{g}"])

        recv_h = exchange(hstate, ["hstate"], 10, hin, ["hin"])
        deferred = [lambda: (recv_h(), fixup())]

        order = [NG - 1] + list(range(NG - 1))
        nslot = prep2(l, order[0], fix=False)
        for oi, g in enumerate(order):
            cols = slice(g * TG, (g + 1) * TG)
            slot = nslot
            xg = xgs[slot]

            for vb in range(2):
                W, wr = wget([(win[:, 3584 + vb * 512: 3584 + (vb + 1) * 512], 512)], 8)
                for tt in range(4):
                    gvt, gvr = A2.f32(tt * 4096 + vb * 2048, 512)
                    ps, pr = nextps()
                    for k in range(8):
                        MM(ps, h[:, k, tt * 128:(tt + 1) * 128], W(k, 0, 512), k == 0, k == 7,
                           [wr, f"h{k}"], [pr])
                    ACT(gvt, ps, AF.Gelu_apprx_tanh, [pr], gvr)
            for ub in range(2):
                W, wr = wget([(win[:, 2560 + ub * 512: 2560 + (ub + 1) * 512], 512)], 8)
                for ji in range(4):
                    j = 4 * ub + ji
                    guj, gur = A1.f32(j * 2048, TG)
                    ps, pr = nextps()
                    for k in range(8):
                        MM(ps, W(k, ji * 128, 128), h[:, k, :], k == 0, k == 7, [wr, f"h{k}"], [pr])
                    ACT(guj, ps, AF.Gelu_apprx_tanh, [pr], gur)
            for tt in range(4):
                gvt, gvr = A2.f32(tt * 4096, 1024)
                vnt, vnr = A3.bf16(tt * 2048, 1024)
                S.op("dve", lambda e, tt=tt, gvt=gvt: e.bn_stats(out=bnst[:, tt, 0:6], in_=gvt[:, 0:512]),
                     gvr, [f"bnst{tt}"])
                S.op("dve", lambda e, tt=tt, gvt=gvt: e.bn_stats(out=bnst[:, tt, 6:12], in_=gvt[:, 512:1024]),
                     gvr, [f"bnst{tt}"])
                S.op("dve", lambda e, tt=tt: e.bn_aggr(out=mv[:, tt, :], in_=bnst[:, tt, :]),
                     [f"bnst{tt}"], [f"mv{tt}"])
                ACT(lsd[:, tt:tt + 1], mv[:, tt, 1:2], AF.Sqrt, [f"mv{tt}", "epsb"], [f"lsd{tt}"],
                    scale=1.0, bias=epsb)
                S.op("dve", lambda e, tt=tt: e.reciprocal(out=lrs[:, tt:tt + 1], in_=lsd[:, tt:tt + 1]),
                     [f"lsd{tt}"], [f"lrs{tt}"])
                TS(vnt, gvt, mv[:, tt, 0:1], lrs[:, tt:tt + 1], ALU.subtract, ALU.mult,
                   gvr + [f"mv{tt}", f"lrs{tt}"], vnr)
            for gi in range(8):
                ps, pr = nextps()
                for tt in range(4):
                    vnt, vnr = A3.bf16(tt * 2048, 1024)
                    MM(ps[:, tt * 128:(tt + 1) * 128], vnt[:, gi * 128:(gi + 1) * 128], wsbf[:, gi, :],
                       True, True, vnr + ["wsbf"], [pr], inc=(tt == 3))
                tm, tmr = tmpm[gi % 2]
                for tt in range(4):
                    STT(tm[:, tt * 128:(tt + 1) * 128], ps[:, tt * 128:(tt + 1) * 128], pvc(l, 88 + gi),
                        Cg[:, gi, :], ALU.mult, ALU.add, [pr, "pv", "Cg"], tmr)
                guj, gur = A1.f32(gi * 2048, TG)
                TT(yb_pre[:, gi, :], tm, guj, ALU.mult, tmr + gur, [f"yb{gi}"])

            while deferred:
                deferred.pop(0)()
            for ab in range(2):
                W, wr = wget([(win[:, 4608 + ab * 512: 4608 + (ab + 1) * 512], 512)], 8)
                for ji in range(4):
                    j = 4 * ab + ji
                    sgj, sgr = A2.f32(j * 2048, TG)
                    ps, pr = nextps()
                    for k in range(8):
                        MM(ps, W(k, ji * 128, 128), h[:, k, :], k == 0, k == 7, [wr, f"h{k}"], [pr])
                    ACT(sgj, ps, AF.Sigmoid, [pr], sgr)
            for ab in range(2):
                W, wr = wget([(wa_d[l][:, ab * 512:(ab + 1) * 512], 512)], 10)
                for ji in range(4):
                    j = 4 * ab + ji
                    sgj, sgr = A2.f32(j * 2048, TG)
                    ps, pr = nextps()
                    for c in range(10):
                        MM(ps, W(c, ji * 128, 128), ya_pre[:, c, :], c == 0, c == 9, [wr, f"ya{c}"], [pr])
                    TT(sgj, ps, sgj, ALU.mult, [pr] + sgr, sgr)
            for ab in range(2):
                W, wr = wget([(win[:, 5632 + ab * 512: 5632 + (ab + 1) * 512], 512)], 8)
                for ji in range(4):
                    j = 4 * ab + ji
                    s2j, s2r = A1.f32(j * 2048, TG)
                    ps, pr = nextps()
                    for k in range(8):
                        MM(ps, W(k, ji * 128, 128), h[:, k, :], k == 0, k == 7, [wr, f"h{k}"], [pr])
                    ACT(s2j, ps, AF.Sigmoid, [pr], s2r)
            for ab in range(2):
                W, wr = wget([(wb_d[l][:, ab * 512:(ab + 1) * 512], 512)], 8)
                for ji in range(4):
                    j = 4 * ab + ji
                    sgj, sgr = A2.f32(j * 2048, TG)
                    s2j, s2r = A1.f32(j * 2048, TG)
                    mj, mr = A3.bf16(j * 1024, TG)
                    ps, pr = nextps()
                    for c in range(8):
                        MM(ps, W(c, ji * 128, 128), yb_pre[:, c, :], c == 0, c == 7, [wr, f"yb{c}"], [pr])
                    TT(s2j, ps, s2j, ALU.mult, [pr] + s2r, s2r)
                    TT(mj, sgj, s2j, ALU.add, sgr + s2r, mr)
            for ob in range(2):
                W, wr = wget([(wo_d[l][:, ob * 512:(ob + 1) * 512], 512)], 8)
                for ji in range(4):
                    j = 4 * ob + ji
                    ps, pr = nextps()
                    for k in range(8):
                        mk, mkr = A3.bf16(k * 1024, TG)
                        MM(ps, W(k, ji * 128, 128), mk, k == 0, k == 7, [wr] + mkr, [pr])
                    TT(xg[:, j, :], xg[:, j, :], ps, ALU.add, [pr, f"xg{slot}_{j}"], [f"xg{slot}_{j}"])

            for k in range(8):
                ACT(h[:, k, :], xg[:, k, :], AF.Copy, [f"xg{slot}_{k}", "pv"], [f"h{k}"], scale=pvc(l, 104 + k))
            rstd2, rstd2_r = A2.f32(0, TG)
            pend = []
            for ub in range(8):
                W, wr = wget([(wu_d[l][:, ub * 512:(ub + 1) * 512], 512)], 8)
                for ji in range(4):
                    jf = 4 * ub + ji
                    fj, fr = FR.bf16(jf * 1024, TG)
                    ps, pr = nextps()
                    for k in range(8):
                        MM(ps, W(k, ji * 128, 128), h[:, k, :], k == 0, k == 7, [wr, f"h{k}"], [pr])
                    rr, rrr = rl[jf % 2]

                    def epi(fj=fj, fr=fr, ps=ps, pr=pr, rr=rr, rrr=rrr):
                        ACT(rr, ps, AF.Relu, [pr], rrr)
                        TT(rr, rr, rr, ALU.mult, rrr, rrr)
                        TT(fj, rr, rstd2, ALU.mult, rrr + rstd2_r, fr)
                    if ub == 0:
                        pend.append(epi)
                    else:
                        epi()
                if ub == 0:
                    rmsnorm(l, None, slot)
                    TT(rstd2, rstd, rstd, ALU.mult, ["rstd"], rstd2_r)
                    for epi in pend:
                        epi()
            if oi + 1 < len(order):
                nslot = prep2(l, order[oi + 1])
            for j in range(8):
                W, wr = wget([(wd_d[l][:, j * 128:(j + 1) * 128], 128)], 32)
                ps, pr = nextps()
                for kf in range(32):
                    fk, fkr = FR.bf16(kf * 1024, TG)
                    MM(ps, W(kf, 0, 128), fk, kf == 0, kf == 31, [wr] + fkr, [pr])
                TT(xg[:, j, :], xg[:, j, :], ps, ALU.add, [pr, f"xg{slot}_{j}"], [f"xg{slot}_{j}"])

            if l < NL - 1:
                dst = xs_d.rearrange("(k p) t -> p k t", p=128)[:, :, cols]
                S.dma("sp", f"st_x{slot}", lambda e, d=dst, xg=xg: e.dma_start(out=d, in_=xg), xres(slot), [f"xs{g}"])
                if g == NG - 1:
                    CP(sq3, xg[:, :, TG - 3:TG], xres(slot), ["sq3"])
                    deferred.append(exchange(sq3.rearrange("p k t -> p (k t)"), ["sq3"], 24,
                                             xh3.rearrange("p k t -> p (k t)"), ["xh3"]))
            else:
                rmsnorm(l, None, slot)
                yres = []
                for k in range(8):
                    yk, ykr = A1.f32(k * 2048, TG)
                    STT(yk, xg[:, k, :], pvc(l, 112 + k), rstd, ALU.mult, ALU.mult,
                        [f"xg{slot}_{k}", "pv", "rstd"], ykr)
                    yres += ykr
                yv = A1.t32.rearrange("p (k t) -> p k t", k=8)
                dst = yT_d.rearrange("(k p) t -> p k t", p=128)[:, :, cols]
                S.dma("sp", "st_y", lambda e, d=dst, yv=yv: e.dma_start(out=d, in_=yv), yres, [f"y{g}"])

        while deferred:
            deferred.pop(0)()

    if dry:
        return recorded
    S.finish(["st_y", "st_x0", "st_x1", "st_y1", "st_y2", "ex"])
    S.emit()
    return nc


def _pack_params(p):
    f = np.float32
    pv = np.zeros((NL, 128, NPV), f)
    gw = np.zeros((NL, 128, 2, 10, 128), f)
    for l in range(NL):
        def put(col0, vec):
            n = vec.shape[0] // 128
            pv[l, :, col0:col0 + n] = vec.reshape(n, 128).T
        put(0, p["norm_mix_g"][l])
        for tap in range(4):
            put(8 + tap * 10, p["conv_w"][l, tap])
        put(48, p["conv_b"][l])
        put(58, p["lru_b_a"][l].reshape(-1))
        put(68, p["lru_b_x"][l].reshape(-1))
        put(78, p["lru_lambda"][l])
        put(88, p["sgu_ln_g"][l])
        put(96, p["sgu_ln_b"][l])
        put(104, p["norm_ffn_g"][l])
        put(112, p["final_norm_g"])
        for hd in range(20):
            c, hh = divmod(hd, 2)
            gw[l, hh * 64:(hh + 1) * 64, 0, c, hh * 64:(hh + 1) * 64] = p["lru_w_a"][l, hd]
            gw[l, hh * 64:(hh + 1) * 64, 1, c, hh * 64:(hh + 1) * 64] = p["lru_w_x"][l, hd]
    wsT = np.ascontiguousarray(np.transpose(p["sgu_w_s"], (0, 3, 1, 2))).reshape(NL, 128, 8 * 128)
    bsb = np.ascontiguousarray(np.broadcast_to(p["sgu_b_s"].reshape(NL, 1, 8 * 128), (NL, 128, 8 * 128)))
    return pv, gw.reshape(NL, 128, 2 * 10 * 128), wsT.astype(f), bsb.astype(f)


_CACHE = {}


def _program(T):
    if T not in _CACHE:
        plan = build(T, None)
        _CACHE[T] = build(T, plan)
    return _CACHE[T]


def kernel(**inputs):
    p = {k: np.asarray(v, dtype=np.float32) for k, v in inputs.items()}
    x = p["x"]
    B, SEQ, _ = x.shape
    pv, gw, wsT, bsb = _pack_params(p)
    n = 8
    halves = n // B
    T = SEQ // halves
    nc = _program(T)
    shared = {
        "w_in": p["w_in"], "w_branch_a": p["w_branch_a"], "w_branch_b": p["w_branch_b"],
        "w_out": p["w_out"], "w_up": p["w_up"], "w_down": p["w_down"],
        "pv": pv, "gw": gw, "wsT": wsT, "bsb": bsb,
    }
    in_maps = []
    for c in range(n):
        b, hf = divmod(c, halves)
        m = dict(shared)
        m["xT"] = np.ascontiguousarray(x[b, hf * T:(hf + 1) * T].T)
        xh = np.zeros((D, 3), np.float32)
        oh = np.zeros((2, 1), np.float32)
        sel = np.zeros((2, 1), np.float32)
        oh[hf, 0] = 1.0
        if hf > 0:
            xh[:] = x[b, hf * T - 3:hf * T].T
            sel[hf - 1, 0] = 1.0
        m["xh"], m["oh"], m["sel"] = xh, oh, sel
        in_maps.append(m)
    res = run_bass_kernel_spmd(nc, in_maps, core_ids=list(range(n)))
    out = np.empty((B, SEQ, D), np.float32)
    for c in range(n):
        b, hf = divmod(c, halves)
        out[b, hf * T:(hf + 1) * T] = res.results[c]["yT"].T
    return out
```

```python
import numpy as np
import concourse.bass as bass
import concourse.mybir as mybir
from concourse.bass_utils import run_bass_kernel_spmd

F32 = mybir.dt.float32
BF16 = mybir.dt.bfloat16
AF = mybir.ActivationFunctionType
ALU = mybir.AluOpType

D = 1024
DR = 1280
DS = 1024
DFF = 4096
DIN = 6656
NL = 2
EPS = 1e-6
TG = 512
NPV = 120
ENGS = ("pe", "act", "dve", "pool", "sp")


class Sched:
    def __init__(self, nc):
        self.nc = nc
        self.q = {e: [] for e in ENGS}
        self.sems = {}
        self.cnt = {}
        self.seen = {e: {} for e in ENGS}
        self.last_w = {}
        self.readers = {}

    def sem(self, key):
        if key not in self.sems:
            self.sems[key] = self.nc.alloc_semaphore(name="s_" + key)
            self.cnt[key] = 0
        return self.sems[key]

    def _deps(self, engine, reads, writes):
        best = {}
        def add(ev):
            sk, v = ev
            if engine == "pe" and sk == "pe":
                return
            if best.get(sk, 0) < v:
                best[sk] = v
        for r in reads:
            ev = self.last_w.get(r)
            if ev is not None:
                add(ev)
        for w in writes:
            ev = self.last_w.get(w)
            if ev is not None:
                add(ev)
            for ev in self.readers.get(w, ()):
                add(ev)
        waits = []
        for sk, v in best.items():
            if self.seen[engine].get(sk, 0) < v:
                self.seen[engine][sk] = v
                waits.append((sk, v))
        return waits

    def _record(self, ev, reads, writes):
        for r in reads:
            self.readers.setdefault(r, []).append(ev)
        for w in writes:
            self.last_w[w] = ev
            self.readers[w] = []

    def op(self, engine, fn, reads=(), writes=(), inc=True):
        self.sem(engine)
        waits = self._deps(engine, reads, writes)
        ev = (engine, self.cnt[engine] + 1)
        if inc:
            self.cnt[engine] += 1
        else:
            assert engine == "pe"
        self._record(ev, reads, writes)
        self.q[engine].append((waits, fn, (engine, 1) if inc else None))

    def dma(self, queue, semkey, fn, reads=(), writes=()):
        self.sem(semkey)
        waits = self._deps(queue, reads, writes)
        self.cnt[semkey] += 16
        ev = (semkey, self.cnt[semkey])
        self._record(ev, reads, writes)
        self.q[queue].append((waits, fn, (semkey, 16)))

    def finish(self, dma_semkeys):
        waits = [(sk, self.cnt[sk]) for sk in dma_semkeys if sk in self.cnt]
        self.q["sp"].append((waits, None, None))

    def emit(self):
        nc = self.nc
        sems = self.sems

        def run(eng, items):
            for waits, fn, inc in items:
                for sk, v in waits:
                    eng.wait_ge(sems[sk], v)
                if fn is None:
                    continue
                ins = fn(eng)
                if inc is not None:
                    ins.then_inc(sems[inc[0]], inc[1])

        with nc.Block() as block:
            @block.sync
            def _(e):
                run(e, self.q["sp"])

            @block.tensor
            def _(e):
                run(e, self.q["pe"])

            @block.scalar
            def _(e):
                run(e, self.q["act"])

            @block.vector
            def _(e):
                run(e, self.q["dve"])

            @block.gpsimd
            def _(e):
                run(e, self.q["pool"])


class Region:
    def __init__(self, nc, name, nbytes):
        self.name = name
        self.t32 = nc.alloc_sbuf_tensor("rg_" + name, [128, nbytes // 4], F32).ap()
        self.t16 = self.t32.bitcast(BF16)

    def _res(self, off, nb):
        return [f"{self.name}.{u}" for u in range(off // 1024, (off + nb + 1023) // 1024)]

    def f32(self, off, n):
        return self.t32[:, off // 4: off // 4 + n], self._res(off, n * 4)

    def bf16(self, off, n):
        return self.t16[:, off // 2: off // 2 + n], self._res(off, n * 2)


def build(T, plan=None):
    dry = plan is None
    NG = T // TG
    nc = bass.Bass("TRN2", target_bir_lowering=False)
    S = Sched(nc)

    def din(name, shape):
        return nc.dram_tensor(name, shape, F32, kind="ExternalInput").ap()

    xT_d = din("xT", [D, T])
    win_d = din("w_in", [NL, D, DIN])
    wa_d = din("w_branch_a", [NL, DR, D])
    wb_d = din("w_branch_b", [NL, DS, D])
    wo_d = din("w_out", [NL, D, D])
    wu_d = din("w_up", [NL, D, DFF])
    wd_d = din("w_down", [NL, DFF, D])
    pv_d = din("pv", [NL, 128, NPV])
    gw_d = din("gw", [NL, 128, 2 * 10 * 128])
    ws_d = din("wsT", [NL, 128, 8 * 128])
    bsb_d = din("bsb", [NL, 128, 8 * 128])
    xh_d = din("xh", [D, 3])
    oh_d = din("oh", [2, 1])
    sel_d = din("sel", [2, 1])
    yT_d = nc.dram_tensor("yT", [D, T], F32, kind="ExternalOutput").ap()
    xs_d = nc.dram_tensor("xs", [D, T], F32, kind="Internal").ap()
    y1_d = nc.dram_tensor("y1s", [DR, T], BF16, kind="Internal").ap()
    y2_d = nc.dram_tensor("y2s", [DR, T], BF16, kind="Internal").ap()

    def sb(name, shape, dt=F32):
        return nc.alloc_sbuf_tensor("sb_" + name, shape, dt).ap()

    ones32 = sb("ones32", [128, 128])
    onesbf = sb("onesbf", [128, 128], BF16)
    epsb = sb("epsb", [128, 1])
    pv = sb("pv", [128, NL, NPV])
    sct = sb("sct", [128, NL, 10])
    sc = sb("sc", [128, NL, 10])
    sc2 = sb("sc2", [128, NL, 10])
    hsc = sb("hsc", [128, NL, 10])
    hba = sb("hba", [128, NL, 10])
    hbx = sb("hbx", [128, NL, 10])
    gwbf = sb("gwbf", [128, 2, 10, 128], BF16)
    wsbf = sb("wsbf", [128, 8, 128], BF16)
    Cg = sb("Cg", [128, 8, 128])
    halo = sb("halo", [128, 10, 3])
    hstate = sb("hstate", [128, 10])
    pstate = sb("pstate", [128, 10])
    hin = sb("hin", [128, 10])
    xh3 = sb("xh3", [128, 8, 3])
    sq3 = sb("sq3", [128, 8, 3])
    rs3 = sb("rs3", [128, 3])
    hh = sb("hh", [128, 8, 3], BF16)
    zer = sb("zer", [128, TG])
    oh = sb("oh", [2, 1])
    sel = sb("sel", [2, 1])
    y2g = sb("y2g", [128, 10, TG], BF16)
    xgs = [sb(f"xg{i}", [128, 8, TG]) for i in range(2)]
    cur = {"slot": 0, "n": 0}
    h = sb("h", [128, 8, TG], BF16)
    h1 = sb("h1", [128, 8, TG], BF16)
    ya_pre = sb("ya_pre", [128, 10, TG], BF16)
    yb_pre = sb("yb_pre", [128, 8, TG], BF16)
    rstd = sb("rstd", [128, TG])
    bnst = sb("bnst", [128, 4, 12])
    mv = sb("mv", [128, 4, 2])
    lsd = sb("lsd", [128, 4])
    lrs = sb("lrs", [128, 4])
    A1 = Region(nc, "A1", 16384)
    A2 = Region(nc, "A2", 16384)
    A3 = Region(nc, "A3", 8192)
    FR = Region(nc, "FR", 32768)
    bsb_v, bsb_r = A2.f32(0, 1024)
    bsb = bsb_v.rearrange("p (g t) -> p g t", g=8)
    acc = [A2.f32(i * 2048, TG) for i in range(2)] + [A2.f32(12288 + i * 2048, TG) for i in range(2)]
    xr_pad = [A2.f32(4096 + i * 3072, TG + 3) for i in range(2)]
    xr_bf = [A2.bf16(10240 + i * 1024, TG) for i in range(2)]
    tmpm = [A2.f32(i * 2048, TG) for i in range(2)]
    rl = [A2.f32(4096 + i * 2048, TG) for i in range(2)]
    WCAP = 5120
    NWB = 4
    wst = [sb(f"wst{i}", [128, WCAP], BF16) for i in range(NWB)]
    pss = [nc.alloc_psum_tensor(f"ps{i}", [128, TG], F32).ap() for i in range(8)]
    psn = [0]

    def nextps():
        i = psn[0] % 8
        psn[0] += 1
        return pss[i], f"ps{i}"

    def MM(out, lhsT, rhs, start, stop, reads, writes, inc=None):
        if inc is None:
            inc = stop
        S.op("pe", lambda e: e.matmul(out, lhsT=lhsT, rhs=rhs, start=start, stop=stop),
             reads, writes, inc=inc)

    def ACT(out, in_, func, reads, writes, **kw):
        S.op("act", lambda e: e.activation(out=out, in_=in_, func=func, **kw), reads, writes)

    def TT(out, in0, in1, op, reads, writes):
        S.op("dve", lambda e: e.tensor_tensor(out=out, in0=in0, in1=in1, op=op), reads, writes)

    def TS(out, in0, s1, s2, op0, op1, reads, writes):
        S.op("dve", lambda e: e.tensor_scalar(out=out, in0=in0, scalar1=s1, scalar2=s2, op0=op0, op1=op1),
             reads, writes)

    def STT(out, in0, scalar, in1, op0, op1, reads, writes):
        S.op("dve", lambda e: e.scalar_tensor_tensor(out=out, in0=in0, scalar=scalar, in1=in1, op0=op0, op1=op1),
             reads, writes)

    def CP(out, in_, reads, writes):
        S.op("dve", lambda e: e.tensor_copy(out=out, in_=in_), reads, writes)

    def MSET(ap, val, writes):
        S.op("dve", lambda e: e.memset(ap, val), (), writes)

    recorded = []
    wstate = {"n": 0, "issued": 0}

    def issue(n):
        pieces, kt, ncols = plan[n]
        bi = n % NWB
        view = wst[bi][:, 0:kt * ncols].rearrange("p (k c) -> p k c", k=kt)
        c0 = 0
        for (src2d, nc_) in pieces:
            dst = view[:, :, c0:c0 + nc_]
            src = src2d.rearrange("(k p) c -> p k c", p=128)
            S.dma("pool", f"w{bi}", lambda e, d=dst, s=src: e.dma_start(out=d, in_=s), (), [f"wst{bi}"])
            c0 += nc_

    def wget(pieces, kt):
        ncols = sum(p[1] for p in pieces)
        assert kt * ncols <= WCAP
        n = wstate["n"]
        wstate["n"] += 1
        if dry:
            recorded.append((pieces, kt, ncols))
            bi = n % NWB
        else:
            while wstate["issued"] < min(n + 3, len(plan)):
                issue(wstate["issued"])
                wstate["issued"] += 1
            bi = n % NWB
        view = wst[bi][:, 0:kt * ncols].rearrange("p (k c) -> p k c", k=kt)

        def W(k, c0, n_):
            return view[:, k, c0:c0 + n_]
        return W, f"wst{bi}"

    def pvc(l, col):
        return pv[:, l, col:col + 1]

    MSET(ones32, 1.0, ["ones32"])
    MSET(onesbf, 1.0, ["onesbf"])
    MSET(epsb, EPS, ["epsb"])
    MSET(zer, 0.0, ["zer"])
    S.dma("sp", "ld_misc", lambda e: e.dma_start(out=oh, in_=oh_d), (), ["oh"])
    S.dma("sp", "ld_sel", lambda e: e.dma_start(out=sel, in_=sel_d), (), ["sel"])
    S.dma("sp", "ld_xh", lambda e: e.dma_start(out=xh3, in_=xh_d.rearrange("(k p) t -> p k t", p=128)),
          (), ["xh3"])
    nex = [0]

    def exchange(src, srcres, f, dst, dstres):
        i = nex[0]
        nex[0] += 1
        n = 128 * f
        scr = nc.dram_tensor(f"exs{i}", [1, n], F32, kind="Internal").ap()
        cin = nc.dram_tensor(f"exi{i}", [2, n], F32, kind="Internal").ap()
        cout = nc.dram_tensor(f"exo{i}", [2, n], F32, kind="Internal").ap()
        G, gr = FR.t32[0:2, 0:n], FR._res(0, n * 4)
        S.dma("sp", "ex", lambda e: e.dma_start(out=scr.rearrange("o (p f) -> (o p) f", p=128), in_=src),
              srcres, [f"exs{i}"])
        S.dma("sp", "ex", lambda e: e.dma_start(out=G, in_=scr.broadcast_to([2, n])), [f"exs{i}"], gr)
        S.op("dve", lambda e: e.tensor_scalar_mul(out=G, in0=G, scalar1=oh[0:2, 0:1]), gr + ["oh"], gr)
        S.dma("sp", "ex", lambda e: e.dma_start(out=cin, in_=G), gr, [f"exi{i}"])
        S.sem("cc")
        waits = S._deps("pool", [f"exi{i}"], [f"exo{i}"])
        for wk in [k_ for k_ in S.cnt if k_ not in ENGS and k_ != "cc"]:
            if S.seen["pool"].get(wk, 0) < S.cnt[wk]:
                S.seen["pool"][wk] = S.cnt[wk]
                waits.append((wk, S.cnt[wk]))
        S.cnt["cc"] += 1
        S._record(("cc", S.cnt["cc"]), [f"exi{i}"], [f"exo{i}"])
        S.q["pool"].append((waits, lambda e: e.collective_compute(
            "AllReduce", ALU.add, replica_groups=[[0, 1], [2, 3], [4, 5], [6, 7]],
            ins=[cin.opt()], outs=[cout.opt()]),
            ("cc", 1)))
        S.seen["pool"]["cc"] = S.cnt["cc"]
        S.q["pool"].append(([("cc", S.cnt["cc"])], None, None))
        def recv():
            S.dma("sp", "ex", lambda e: e.dma_start(out=G, in_=cout), [f"exo{i}"], gr)
            Gv = G.rearrange("r (p f) -> r p f", p=128)
            ps, pr = nextps()
            for j in range(f):
                MM(ps[:, j:j + 1], Gv[:, :, j], sel[0:2, 0:1], True, True, gr + ["sel"], [pr], inc=(j == f - 1))
            CP(dst, ps[:, 0:f], [pr], dstres)
        return recv
    S.dma("sp", "ld_pv", lambda e: e.dma_start(out=pv, in_=pv_d.rearrange("l p n -> p l n")), (), ["pv"])
    ACT(sct, pv[:, :, 78:88], AF.Exp, ["pv"], ["sct"], scale=-1.0)
    ACT(sct, sct, AF.Ln, ["sct"], ["sct"], scale=1.0, bias=1.0)
    S.op("dve", lambda e: e.tensor_scalar_mul(out=sc, in0=sct, scalar1=-8.0), ["sct"], ["sc"])
    S.op("dve", lambda e: e.tensor_scalar_mul(out=sc2, in0=sct, scalar1=-16.0), ["sct"], ["sc2"])
    S.op("dve", lambda e: e.tensor_scalar_mul(out=hsc, in0=sct, scalar1=-4.0), ["sct"], ["hsc"])
    S.op("dve", lambda e: e.tensor_scalar_mul(out=hba, in0=pv[:, :, 58:68], scalar1=0.5), ["pv"], ["hba"])
    S.op("dve", lambda e: e.tensor_scalar_mul(out=hbx, in0=pv[:, :, 68:78], scalar1=0.5), ["pv"], ["hbx"])

    def rms_sq(slot):
        xg = xgs[slot]
        for k in range(8):
            sqk, r = A1.f32(k * 2048, TG)
            ACT(sqk, xg[:, k, :], AF.Square, [f"xg{slot}_{k}"], r)

    def rmsnorm(l, gbase, slot, hb=None, bank=None, squares=True):
        xg = xgs[slot]
        if squares:
            rms_sq(slot)
        ps, pr = nextps() if bank is None else (pss[bank], f"ps{bank}")
        for k in range(8):
            sqk, r = A1.f32(k * 2048, TG)
            MM(ps, ones32, sqk, k == 0, k == 7, ["ones32"] + r, [pr])
        ACT(rstd, ps, AF.Sqrt, [pr, "epsb"], ["rstd"], scale=1.0 / D, bias=epsb)
        S.op("dve", lambda e: e.reciprocal(out=rstd, in_=rstd), ["rstd"], ["rstd"])
        if gbase is not None:
            ht, hp = hb if hb is not None else (h, "h")
            for k in range(8):
                STT(ht[:, k, :], xg[:, k, :], pvc(l, gbase + k), rstd, ALU.mult, ALU.mult,
                    [f"xg{slot}_{k}", "pv", "rstd"], [f"{hp}{k}"])

    hres = [f"h{k}" for k in range(8)]

    def load_x(l, g):
        cur["n"] += 1
        slot = cur["n"] % 2
        cols = slice(g * TG, (g + 1) * TG)
        src = (xT_d if l == 0 else xs_d).rearrange("(k p) t -> p k t", p=128)[:, :, cols]
        S.dma("sp", f"ld_x{slot}", lambda e, s=src, slot=slot: e.dma_start(out=xgs[slot], in_=s),
              ([] if l == 0 else [f"xs{g}"]), xres(slot))
        return slot

    def xres(slot):
        return [f"xg{slot}_{k}" for k in range(8)]

    def prep2(l, g, fix=True):
        slot = load_x(l, g)
        rmsnorm(l, 0, slot)
        cols = slice(g * TG, (g + 1) * TG)
        s1 = y1_d.rearrange("(c p) t -> p c t", p=128)[:, :, cols]
        s2 = y2_d.rearrange("(c p) t -> p c t", p=128)[:, :, cols]
        S.dma("sp", "ld_y1", lambda e, s_=s1: e.dma_start(out=ya_pre, in_=s_), [f"y1d{g}"], yares)
        S.dma("sp", "ld_y2", lambda e, s_=s2: e.dma_start(out=y2g, in_=s_), [f"y2d{g}"], y2res)
        if fix:
            fixup()
        return slot

    def fixup():
        for c in range(10):
            STT(ya_pre[:, c, :], y2g[:, c, :], hin[:, c:c + 1], ya_pre[:, c, :], ALU.mult, ALU.add,
                [f"y2{c}", "hin", f"ya{c}"], [f"ya{c}"])
    yares = [f"ya{c}" for c in range(10)]
    y2res = [f"y2{c}" for c in range(10)]

    for l in range(NL):
        S.dma("pool", "ld_gw", lambda e, l=l: e.dma_start(
            out=gwbf.rearrange("p a c m -> p (a c m)"), in_=gw_d[l]), (), ["gwbf"])
        S.dma("pool", "ld_ws", lambda e, l=l: e.dma_start(
            out=wsbf.rearrange("p g t -> p (g t)"), in_=ws_d[l]), (), ["wsbf"])
        S.dma("sp", "ld_bsb", lambda e, l=l: e.dma_start(out=bsb_v, in_=bsb_d[l]), (), bsb_r)
        MSET(wsbf[64:128, :, 0:64], 0.0, ["wsbf"])
        for g in range(8):
            ps, pr = nextps()
            MM(ps[:, 0:128], onesbf, wsbf[:, g, :], True, True, ["onesbf", "wsbf"], [pr])
            STT(Cg[:, g, :], ps[:, 0:128], pvc(l, 96 + g), bsb[:, g, :], ALU.mult, ALU.add,
                [pr, "pv"] + bsb_r, ["Cg"])
        MSET(hstate, 0.0, ["hstate"])
        MSET(pstate, 1.0, ["pstate"])
        win = win_d[l]

        S.op("act", lambda e: e.activation(out=sq3, in_=xh3, func=AF.Square), ["xh3"], ["sq3"])
        ps, pr = nextps()
        for k in range(8):
            MM(ps[:, 0:3], ones32, sq3[:, k, :], k == 0, k == 7, ["ones32", "sq3"], [pr])
        ACT(rs3, ps[:, 0:3], AF.Sqrt, [pr, "epsb"], ["rs3"], scale=1.0 / D, bias=epsb)
        S.op("dve", lambda e: e.reciprocal(out=rs3, in_=rs3), ["rs3"], ["rs3"])
        for k in range(8):
            STT(hh[:, k, :], xh3[:, k, :], pvc(l, k), rs3, ALU.mult, ALU.mult, ["xh3", "pv", "rs3"], ["hh"])

        for cp_ in range(5):
            W, wr = wget([(win[:, cp_ * 256: cp_ * 256 + 256], 256)], 8)
            for ci in range(2):
                c = 2 * cp_ + ci
                psh, phr = nextps()
                for k in range(8):
                    MM(psh[:, 0:3], W(k, ci * 128, 128), hh[:, k, :], k == 0, k == 7, [wr, "hh"], [phr])
                CP(halo[:, c, :], psh[:, 0:3], [phr], [f"halo{c}"])

        for g in range(NG):
            cols = slice(g * TG, (g + 1) * TG)
            hbs = [(h, "h"), (h1, "hB")]
            hcur, hpre = hbs[g % 2]
            if g == 0:
                rmsnorm(l, 0, load_x(l, g), hbs[0])
            st = {}

            def bufs(c):
                par = c % 2
                def tmp(idx, par=par):
                    return FR.f32((par * 8 + idx) * 2048, TG)
                return dict(r=tmp(0), i=tmp(1), a=tmp(2), a2=tmp(3), nrm=tmp(4), u=tmp(5), hs=tmp(6), gg=tmp(7),
                            xp=xr_pad[par], ac=acc[c % 4], xb=xr_bf[par])

            def S1(c):
                cp_, ci = divmod(c, 2)
                if ci == 0:
                    st[("W", cp_)] = wget([(win[:, cp_ * 256: cp_ * 256 + 256], 256),
                                           (win[:, DR + cp_ * 256: DR + cp_ * 256 + 256], 256)], 8)
                W, wr = st[("W", cp_)]
                ps1, p1r = pss[c % 2], f"ps{c % 2}"
                for k in range(8):
                    MM(ps1, W(k, ci * 128, 128), hcur[:, k, :], k == 0, k == 7, [wr, f"{hpre}{k}"], [p1r])
                st[("ps1", c)] = (ps1, p1r)

            def S2a(c):
                b = bufs(c)
                xp, xpr = b["xp"]; ac, acr = b["ac"]
                ps1, p1r = st[("ps1", c)]
                ACT(xp[:, 3:TG + 3], ps1, AF.Copy, [p1r], xpr)
                CP(xp[:, 0:3], halo[:, c, :], [f"halo{c}"], xpr)
                TS(ac, xp[:, 3:TG + 3], pvc(l, 8 + 30 + c), pvc(l, 48 + c), ALU.mult, ALU.add,
                   xpr + ["pv"], acr)
                for tap in (2, 1, 0):
                    STT(ac, xp[:, tap:tap + TG], pvc(l, 8 + tap * 10 + c), ac, ALU.mult, ALU.add,
                        xpr + ["pv"] + acr, acr)
                CP(halo[:, c, :], xp[:, TG:TG + 3], xpr, [f"halo{c}"])

            def S2b(c):
                b = bufs(c)
                ac, acr = b["ac"]; xb, xbr = b["xb"]
                ACT(xb, ac, AF.Copy, acr, xbr)

            def S3(c):
                cp_, ci = divmod(c, 2)
                W, wr = st[("W", cp_)]
                b = bufs(c)
                xb, xbr = b["xb"]
                psa, par_ = pss[2 + c % 2], f"ps{2 + c % 2}"
                MM(psa, gwbf[:, 0, c, :], xb, True, True, ["gwbf"] + xbr, [par_])
                psx, pxr = pss[4 + c % 2], f"ps{4 + c % 2}"
                MM(psx, gwbf[:, 1, c, :], xb, True, True, ["gwbf"] + xbr, [pxr])
                psg, pgr = pss[6 + c % 2], f"ps{6 + c % 2}"
                for k in range(8):
                    MM(psg, W(k, 256 + ci * 128, 128), hcur[:, k, :], k == 0, k == 7, [wr, f"{hpre}{k}"], [pgr])
                st[("g", c)] = (psa, par_, psx, pxr, psg, pgr)

            def S4a_te(c):
                b = bufs(c)
                psa, par_, psx, pxr, psg, pgr = st[("g", c)]
                r_, r_r = b["r"]; i_, i_r = b["i"]; a_, a_r = b["a"]; a2_, a2_r = b["a2"]
                ACT(r_, psa, AF.Tanh, [par_, "hba"], r_r, scale=0.5, bias=hba[:, l, c:c + 1])
                ACT(i_, psx, AF.Tanh, [pxr, "hbx"], i_r, scale=0.5, bias=hbx[:, l, c:c + 1])
                ACT(a_, r_, AF.Exp, r_r + ["hsc"], a_r, scale=hsc[:, l, c:c + 1], bias=hsc[:, l, c:c + 1])
                ACT(a2_, r_, AF.Exp, r_r + ["sc"], a2_r, scale=sc[:, l, c:c + 1], bias=sc[:, l, c:c + 1])

            def S4a_s(c):
                b = bufs(c)
                a2_, a2_r = b["a2"]; nrm, nrm_r = b["nrm"]
                ACT(nrm, a2_, AF.Sqrt, a2_r, nrm_r, scale=-0.25, bias=0.25)

            def S4a_g(c):
                b = bufs(c)
                psa, par_, psx, pxr, psg, pgr = st[("g", c)]
                gg, gg_r = b["gg"]
                ACT(gg, psg, AF.Gelu_apprx_tanh, [pgr], gg_r)

            def S4a_pair(c0):
                for fn in (S4a_te, S4a_s, S4a_g):
                    fn(c0)
                    fn(c0 + 1)

            def S4b(c):
                b = bufs(c)
                i_, i_r = b["i"]; a_, a_r = b["a"]; a2_, a2_r = b["a2"]
                nrm, nrm_r = b["nrm"]; u_, u_r = b["u"]; hs, hs_r = b["hs"]; gg, gg_r = b["gg"]
                ac, acr = b["ac"]
                STT(u_, i_, 1.0, nrm, ALU.add, ALU.mult, nrm_r + i_r, u_r)
                TT(u_, u_, ac, ALU.mult, u_r + acr, u_r)
                S.op("dve", lambda e: e.tensor_tensor_scan(
                    out=hs, data0=a_, data1=u_, initial=hstate[:, c:c + 1], op0=ALU.mult, op1=ALU.add),
                    a_r + u_r + ["hstate"], hs_r)
                CP(hstate[:, c:c + 1], hs[:, TG - 1:TG], hs_r, ["hstate"])
                pc, pc_r = a2_, a2_r
                S.op("dve", lambda e: e.tensor_tensor_scan(
                    out=pc, data0=a_, data1=zer, initial=pstate[:, c:c + 1], op0=ALU.mult, op1=ALU.add),
                    a_r + ["zer", "pstate"], pc_r)
                CP(pstate[:, c:c + 1], pc[:, TG - 1:TG], pc_r, ["pstate"])
                TT(ya_pre[:, c, :], hs, gg, ALU.mult, hs_r + gg_r, [f"ya{c}"])
                TT(y2g[:, c, :], pc, gg, ALU.mult, pc_r + gg_r, [f"y2{c}"])

            S1(0)
            S1(1)
            for p_ in range(5):
                c0, c1 = 2 * p_, 2 * p_ + 1
                S2a(c0)
                S2a(c1)
                if g + 1 < NG:
                    if p_ == 0:
                        st["nslot"] = load_x(l, g + 1)
                    elif p_ == 2:
                        rms_sq(st["nslot"])
                    elif p_ == 3:
                        rmsnorm(l, 0, st["nslot"], hbs[(g + 1) % 2], bank=0, squares=False)
                if c0 + 2 < 10:
                    S1(c0 + 2)
                    S1(c1 + 2)
                if p_ >= 1:
                    S4a_pair(c0 - 2)
                S2b(c0)
                S2b(c1)
                S3(c0)
                S3(c1)
                if p_ >= 1:
                    S4b(c0 - 2)
                    S4b(c1 - 2)
            S4a_pair(8)
            S4b(8)
            S4b(9)
            d1 = y1_d.rearrange("(c p) t -> p c t", p=128)[:, :, cols]
            d2 = y2_d.rearrange("(c p) t -> p c t", p=128)[:, :, cols]
            S.dma("sp", "st_y1", lambda e, d=d1: e.dma_start(out=d, in_=ya_pre), yares, [f"y1d{g}"])
            S.dma("sp", "st_y2", lambda e, d=d2: e.dma_start(out=d, in_=y2g), y2res, [f"y2d{g}"])

        recv_h = exchange(hstate, ["hstate"], 10, hin, ["hin"])
        deferred = [lambda: (recv_h(), fixup())]

        order = [NG - 1] + list(range(NG - 1))
        nslot = prep2(l, order[0], fix=False)
        for oi, g in enumerate(order):
            cols = slice(g * TG, (g + 1) * TG)
            slot = nslot
            xg = xgs[slot]

            for vb in range(2):
                W, wr = wget([(win[:, 3584 + vb * 512: 3584 + (vb + 1) * 512], 512)], 8)
                for tt in range(4):
                    gvt, gvr = A2.f32(tt * 4096 + vb * 2048, 512)
                    ps, pr = nextps()
                    for k in range(8):
                        MM(ps, h[:, k, tt * 128:(tt + 1) * 128], W(k, 0, 512), k == 0, k == 7,
                           [wr, f"h{k}"], [pr])
                    ACT(gvt, ps, AF.Gelu_apprx_tanh, [pr], gvr)
            for ub in range(2):
                W, wr = wget([(win[:, 2560 + ub * 512: 2560 + (ub + 1) * 512], 512)], 8)
                for ji in range(4):
                    j = 4 * ub + ji
                    guj, gur = A1.f32(j * 2048, TG)
                    ps, pr = nextps()
                    for k in range(8):
                        MM(ps, W(k, ji * 128, 128), h[:, k, :], k == 0, k == 7, [wr, f"h{k}"], [pr])
                    ACT(guj, ps, AF.Gelu_apprx_tanh, [pr], gur)
            for tt in range(4):
                gvt, gvr = A2.f32(tt * 4096, 1024)
                vnt, vnr = A3.bf16(tt * 2048, 1024)
                S.op("dve", lambda e, tt=tt, gvt=gvt: e.bn_stats(out=bnst[:, tt, 0:6], in_=gvt[:, 0:512]),
                     gvr, [f"bnst{tt}"])
                S.op("dve", lambda e, tt=tt, gvt=gvt: e.bn_stats(out=bnst[:, tt, 6:12], in_=gvt[:, 512:1024]),
                     gvr, [f"bnst{tt}"])
                S.op("dve", lambda e, tt=tt: e.bn_aggr(out=mv[:, tt, :], in_=bnst[:, tt, :]),
                     [f"bnst{tt}"], [f"mv{tt}"])
                ACT(lsd[:, tt:tt + 1], mv[:, tt, 1:2], AF.Sqrt, [f"mv{tt}", "epsb"], [f"lsd{tt}"],
                    scale=1.0, bias=epsb)
                S.op("dve", lambda e, tt=tt: e.reciprocal(out=lrs[:, tt:tt + 1], in_=lsd[:, tt:tt + 1]),
                     [f"lsd{tt}"], [f"lrs{tt}"])
                TS(vnt, gvt, mv[:, tt, 0:1], lrs[:, tt:tt + 1], ALU.subtract, ALU.mult,
                   gvr + [f"mv{tt}", f"lrs{tt}"], vnr)
            for gi in range(8):
                ps, pr = nextps()
                for tt in range(4):
                    vnt, vnr = A3.bf16(tt * 2048, 1024)
                    MM(ps[:, tt * 128:(tt + 1) * 128], vnt[:, gi * 128:(gi + 1) * 128], wsbf[:, gi, :],
                       True, True, vnr + ["wsbf"], [pr], inc=(tt == 3))
                tm, tmr = tmpm[gi % 2]
                for tt in range(4):
                    STT(tm[:, tt * 128:(tt + 1) * 128], ps[:, tt * 128:(tt + 1) * 128], pvc(l, 88 + gi),
                        Cg[:, gi, :], ALU.mult, ALU.add, [pr, "pv", "Cg"], tmr)
                guj, gur = A1.f32(gi * 2048, TG)
                TT(yb_pre[:, gi, :], tm, guj, ALU.mult, tmr + gur, [f"yb{gi}"])

            while deferred:
                deferred.pop(0)()
            for ab in range(2):
                W, wr = wget([(win[:, 4608 + ab * 512: 4608 + (ab + 1) * 512], 512)], 8)
                for ji in range(4):
                    j = 4 * ab + ji
                    sgj, sgr = A2.f32(j * 2048, TG)
                    ps, pr = nextps()
                    for k in range(8):
                        MM(ps, W(k, ji * 128, 128), h[:, k, :], k == 0, k == 7, [wr, f"h{k}"], [pr])
                    ACT(sgj, ps, AF.Sigmoid, [pr], sgr)
            for ab in range(2):
                W, wr = wget([(wa_d[l][:, ab * 512:(ab + 1) * 512], 512)], 10)
                for ji in range(4):
                    j = 4 * ab + ji
                    sgj, sgr = A2.f32(j * 2048, TG)
                    ps, pr = nextps()
                    for c in range(10):
                        MM(ps, W(c, ji * 128, 128), ya_pre[:, c, :], c == 0, c == 9, [wr, f"ya{c}"], [pr])
                    TT(sgj, ps, sgj, ALU.mult, [pr] + sgr, sgr)
            for ab in range(2):
                W, wr = wget([(win[:, 5632 + ab * 512: 5632 + (ab + 1) * 512], 512)], 8)
                for ji in range(4):
                    j = 4 * ab + ji
                    s2j, s2r = A1.f32(j * 2048, TG)
                    ps, pr = nextps()
                    for k in range(8):
                        MM(ps, W(k, ji * 128, 128), h[:, k, :], k == 0, k == 7, [wr, f"h{k}"], [pr])
                    ACT(s2j, ps, AF.Sigmoid, [pr], s2r)
            for ab in range(2):
                W, wr = wget([(wb_d[l][:, ab * 512:(ab + 1) * 512], 512)], 8)
                for ji in range(4):
                    j = 4 * ab + ji
                    sgj, sgr = A2.f32(j * 2048, TG)
                    s2j, s2r = A1.f32(j * 2048, TG)
                    mj, mr = A3.bf16(j * 1024, TG)
                    ps, pr = nextps()
                    for c in range(8):
                        MM(ps, W(c, ji * 128, 128), yb_pre[:, c, :], c == 0, c == 7, [wr, f"yb{c}"], [pr])
                    TT(s2j, ps, s2j, ALU.mult, [pr] + s2r, s2r)
                    TT(mj, sgj, s2j, ALU.add, sgr + s2r, mr)
            for ob in range(2):
                W, wr = wget([(wo_d[l][:, ob * 512:(ob + 1) * 512], 512)], 8)
                for ji in range(4):
                    j = 4 * ob + ji
                    ps, pr = nextps()
                    for k in range(8):
                        mk, mkr = A3.bf16(k * 1024, TG)
                        MM(ps, W(k, ji * 128, 128), mk, k == 0, k == 7, [wr] + mkr, [pr])
                    TT(xg[:, j, :], xg[:, j, :], ps, ALU.add, [pr, f"xg{slot}_{j}"], [f"xg{slot}_{j}"])

            for k in range(8):
                ACT(h[:, k, :], xg[:, k, :], AF.Copy, [f"xg{slot}_{k}", "pv"], [f"h{k}"], scale=pvc(l, 104 + k))
            rstd2, rstd2_r = A2.f32(0, TG)
            pend = []
            for ub in range(8):
                W, wr = wget([(wu_d[l][:, ub * 512:(ub + 1) * 512], 512)], 8)
                for ji in range(4):
                    jf = 4 * ub + ji
                    fj, fr = FR.bf16(jf * 1024, TG)
                    ps, pr = nextps()
                    for k in range(8):
                        MM(ps, W(k, ji * 128, 128), h[:, k, :], k == 0, k == 7, [wr, f"h{k}"], [pr])
                    rr, rrr = rl[jf % 2]

                    def epi(fj=fj, fr=fr, ps=ps, pr=pr, rr=rr, rrr=rrr):
                        ACT(rr, ps, AF.Relu, [pr], rrr)
                        TT(rr, rr, rr, ALU.mult, rrr, rrr)
                        TT(fj, rr, rstd2, ALU.mult, rrr + rstd2_r, fr)
                    if ub == 0:
                        pend.append(epi)
                    else:
                        epi()
                if ub == 0:
                    rmsnorm(l, None, slot)
                    TT(rstd2, rstd, rstd, ALU.mult, ["rstd"], rstd2_r)
                    for epi in pend:
                        epi()
            if oi + 1 < len(order):
                nslot = prep2(l, order[oi + 1])
            for j in range(8):
                W, wr = wget([(wd_d[l][:, j * 128:(j + 1) * 128], 128)], 32)
                ps, pr = nextps()
                for kf in range(32):
                    fk, fkr = FR.bf16(kf * 1024, TG)
                    MM(ps, W(kf, 0, 128), fk, kf == 0, kf == 31, [wr] + fkr, [pr])
                TT(xg[:, j, :], xg[:, j, :], ps, ALU.add, [pr, f"xg{slot}_{j}"], [f"xg{slot}_{j}"])

            if l < NL - 1:
                dst = xs_d.rearrange("(k p) t -> p k t", p=128)[:, :, cols]
                S.dma("sp", f"st_x{slot}", lambda e, d=dst, xg=xg: e.dma_start(out=d, in_=xg), xres(slot), [f"xs{g}"])
                if g == NG - 1:
                    CP(sq3, xg[:, :, TG - 3:TG], xres(slot), ["sq3"])
                    deferred.append(exchange(sq3.rearrange("p k t -> p (k t)"), ["sq3"], 24,
                                             xh3.rearrange("p k t -> p (k t)"), ["xh3"]))
            else:
                rmsnorm(l, None, slot)
                yres = []
                for k in range(8):
                    yk, ykr = A1.f32(k * 2048, TG)
                    STT(yk, xg[:, k, :], pvc(l, 112 + k), rstd, ALU.mult, ALU.mult,
                        [f"xg{slot}_{k}", "pv", "rstd"], ykr)
                    yres += ykr
                yv = A1.t32.rearrange("p (k t) -> p k t", k=8)
                dst = yT_d.rearrange("(k p) t -> p k t", p=128)[:, :, cols]
                S.dma("sp", "st_y", lambda e, d=dst, yv=yv: e.dma_start(out=d, in_=yv), yres, [f"y{g}"])

        while deferred:
            deferred.pop(0)()

    if dry:
        return recorded
    S.finish(["st_y", "st_x0", "st_x1", "st_y1", "st_y2", "ex"])
    S.emit()
    return nc


def _pack_params(p):
    f = np.float32
    pv = np.zeros((NL, 128, NPV), f)
    gw = np.zeros((NL, 128, 2, 10, 128), f)
    for l in range(NL):
        def put(col0, vec):
            n = vec.shape[0] // 128
            pv[l, :, col0:col0 + n] = vec.reshape(n, 128).T
        put(0, p["norm_mix_g"][l])
        for tap in range(4):
            put(8 + tap * 10, p["conv_w"][l, tap])
        put(48, p["conv_b"][l])
        put(58, p["lru_b_a"][l].reshape(-1))
        put(68, p["lru_b_x"][l].reshape(-1))
        put(78, p["lru_lambda"][l])
        put(88, p["sgu_ln_g"][l])
        put(96, p["sgu_ln_b"][l])
        put(104, p["norm_ffn_g"][l])
        put(112, p["final_norm_g"])
        for hd in range(20):
            c, hh = divmod(hd, 2)
            gw[l, hh * 64:(hh + 1) * 64, 0, c, hh * 64:(hh + 1) * 64] = p["lru_w_a"][l, hd]
            gw[l, hh * 64:(hh + 1) * 64, 1, c, hh * 64:(hh + 1) * 64] = p["lru_w_x"][l, hd]
    wsT = np.ascontiguousarray(np.transpose(p["sgu_w_s"], (0, 3, 1, 2))).reshape(NL, 128, 8 * 128)
    bsb = np.ascontiguousarray(np.broadcast_to(p["sgu_b_s"].reshape(NL, 1, 8 * 128), (NL, 128, 8 * 128)))
    return pv, gw.reshape(NL, 128, 2 * 10 * 128), wsT.astype(f), bsb.astype(f)


_CACHE = {}


def _program(T):
    if T not in _CACHE:
        plan = build(T, None)
        _CACHE[T] = build(T, plan)
    return _CACHE[T]


def kernel(**inputs):
    p = {k: np.asarray(v, dtype=np.float32) for k, v in inputs.items()}
    x = p["x"]
    B, SEQ, _ = x.shape
    pv, gw, wsT, bsb = _pack_params(p)
    n = 8
    halves = n // B
    T = SEQ // halves
    nc = _program(T)
    shared = {
        "w_in": p["w_in"], "w_branch_a": p["w_branch_a"], "w_branch_b": p["w_branch_b"],
        "w_out": p["w_out"], "w_up": p["w_up"], "w_down": p["w_down"],
        "pv": pv, "gw": gw, "wsT": wsT, "bsb": bsb,
    }
    in_maps = []
    for c in range(n):
        b, hf = divmod(c, halves)
        m = dict(shared)
        m["xT"] = np.ascontiguousarray(x[b, hf * T:(hf + 1) * T].T)
        xh = np.zeros((D, 3), np.float32)
        oh = np.zeros((2, 1), np.float32)
        sel = np.zeros((2, 1), np.float32)
        oh[hf, 0] = 1.0
        if hf > 0:
            xh[:] = x[b, hf * T - 3:hf * T].T
            sel[hf - 1, 0] = 1.0
        m["xh"], m["oh"], m["sel"] = xh, oh, sel
        in_maps.append(m)
    res = run_bass_kernel_spmd(nc, in_maps, core_ids=list(range(n)))
    out = np.empty((B, SEQ, D), np.float32)
    for c in range(n):
        b, hf = divmod(c, halves)
        out[b, hf * T:(hf + 1) * T] = res.results[c]["yT"].T
    return out
```
